# Optimizing a Trainium2 kernel written in Bass

```python
import math
import jax, jax.numpy as jnp
from jax import lax
import numpy as np

D_MODEL = 1024
BATCH = 2
SEQ = 8192
DEPTH = 2

N_META = 16
BLOCK = 128
PAD = BLOCK - N_META

D_RNN = D_MODEL
LRU_BLOCKS = 8
LRU_BS = D_RNN // LRU_BLOCKS
LRU_C = 8.0
CONV_A = 4

N_Q_HEADS = 16
N_KV_HEADS = 2
HEAD_DIM = 64
Q_PER_KV = N_Q_HEADS // N_KV_HEADS
WINDOW = 128
Q_DIM = N_Q_HEADS * HEAD_DIM
KV_DIM = N_KV_HEADS * HEAD_DIM

EVEN_IN = 2 * D_RNN + Q_DIM + 2 * KV_DIM
EVEN_MIX = D_RNN + Q_DIM

D_SSM = 2 * D_MODEL
SSD_HEADDIM = 64
SSD_HEADS = D_SSM // SSD_HEADDIM
SSD_GROUPS = 8
SSD_HPG = SSD_HEADS // SSD_GROUPS
SSD_STATE = 128
CONV_C = 4
SSD_CONV_DIM = D_SSM + 2 * SSD_GROUPS * SSD_STATE
ODD_IN = D_SSM + SSD_CONV_DIM + SSD_HEADS

D_FF = 2816
CONV_F = 3

EPS = 1e-6

kernel_name = "hybrid_rglru_swa_sink_ssd_convffn"


def rms_norm(x, w):
    x32 = x.astype(jnp.float32)
    y = x32 * lax.rsqrt(jnp.mean(x32 * x32, axis=-1, keepdims=True) + EPS)
    return (y * w.astype(jnp.float32)).astype(x.dtype)


def causal_dwconv(x, w, b):
    k = w.shape[0]
    y = lax.conv_general_dilated(
        x, w[:, None, :].astype(x.dtype), window_strides=(1,), padding=[(k - 1, 0)],
        dimension_numbers=("NWC", "WIO", "NWC"), feature_group_count=x.shape[-1])
    return y + b.astype(x.dtype)


def alibi_slopes(n_heads):
    return 2.0 ** (-8.0 * jnp.arange(1, n_heads + 1, dtype=jnp.float32) / n_heads)


def rg_lru(x, w_a, b_a, w_x, b_x, lam):
    bsz, L, _ = x.shape
    x32 = x.astype(jnp.float32)
    xb = x32.reshape(bsz, L, LRU_BLOCKS, LRU_BS)
    r = jax.nn.sigmoid(jnp.einsum("blni,nij->blnj", xb, w_a.astype(jnp.float32)).reshape(bsz, L, D_RNN) + b_a)
    i = jax.nn.sigmoid(jnp.einsum("blni,nij->blnj", xb, w_x.astype(jnp.float32)).reshape(bsz, L, D_RNN) + b_x)
    log_a = -LRU_C * r * jax.nn.softplus(-lam.astype(jnp.float32))
    a = jnp.exp(log_a)
    u = jnp.sqrt(-jnp.expm1(2.0 * log_a)) * (i * x32)

    def combine(c1, c2):
        a1, b1 = c1
        a2, b2 = c2
        return a1 * a2, a2 * b1 + b2

    _, h = lax.associative_scan(combine, (a, u), axis=1)
    return h.astype(x.dtype)


def swa_sink_alibi(q, k, v, sinks):
    bsz, L = q.shape[:2]
    Lp = L + PAD
    nblk = Lp // BLOCK
    q = q.astype(jnp.float32)
    k = k.astype(jnp.float32)
    v = v.astype(jnp.float32)
    padw = ((0, 0), (PAD, 0), (0, 0), (0, 0))
    qb = jnp.pad(q, padw).reshape(bsz, nblk, BLOCK, N_KV_HEADS, Q_PER_KV, HEAD_DIM)
    kb = jnp.pad(k, padw).reshape(bsz, nblk, BLOCK, N_KV_HEADS, HEAD_DIM)
    vb = jnp.pad(v, padw).reshape(bsz, nblk, BLOCK, N_KV_HEADS, HEAD_DIM)
    shift = ((0, 0), (1, 0), (0, 0), (0, 0), (0, 0))
    k_band = jnp.concatenate([jnp.pad(kb, shift)[:, :-1], kb], axis=2)
    v_band = jnp.concatenate([jnp.pad(vb, shift)[:, :-1], vb], axis=2)
    k_meta = k[:, :N_META]
    v_meta = v[:, :N_META]
    scale = HEAD_DIM ** -0.5
    s_band = jnp.einsum("bnqkgd,bnskd->bnkgqs", qb, k_band) * scale
    s_meta = jnp.einsum("bnqkgd,bmkd->bnkgqm", qb, k_meta) * scale

    blk = jnp.arange(nblk)
    t = blk[:, None] * BLOCK + jnp.arange(BLOCK)[None, :] - PAD
    s = (blk[:, None] - 1) * BLOCK + jnp.arange(2 * BLOCK)[None, :] - PAD
    dist_band = t[:, :, None] - s[:, None, :]
    band_ok = (s[:, None, :] >= N_META) & (dist_band >= 0) & (dist_band < WINDOW)
    dist_meta = t[:, :, None] - jnp.arange(N_META)[None, None, :]
    meta_ok = dist_meta >= 0

    slopes = alibi_slopes(N_Q_HEADS).reshape(N_KV_HEADS, Q_PER_KV)[:, :, None, None]
    pen_band = slopes * dist_band[:, None, None].astype(jnp.float32)
    pen_meta = slopes * jnp.minimum(dist_meta, WINDOW)[:, None, None].astype(jnp.float32)
    s_band = jnp.where(band_ok[:, None, None], s_band - pen_band, -jnp.inf)
    s_meta = jnp.where(meta_ok[:, None, None], s_meta - pen_meta, -jnp.inf)

    sink = sinks.astype(jnp.float32).reshape(N_KV_HEADS, Q_PER_KV)[:, :, None, None]
    mx = jnp.maximum(jnp.maximum(s_band.max(-1, keepdims=True), s_meta.max(-1, keepdims=True)), sink)
    p_band = jnp.exp(s_band - mx)
    p_meta = jnp.exp(s_meta - mx)
    denom = p_band.sum(-1, keepdims=True) + p_meta.sum(-1, keepdims=True) + jnp.exp(sink - mx)
    p_band = p_band / denom
    p_meta = p_meta / denom
    o = (jnp.einsum("bnkgqs,bnskd->bnqkgd", p_band, v_band)
         + jnp.einsum("bnkgqm,bmkd->bnqkgd", p_meta, v_meta))
    return o.reshape(bsz, Lp, Q_DIM)[:, PAD:]


def griffin_swa_mixer(u, w_in, conv_w, conv_b, w_a, b_a, w_x, b_x, lam, sinks, w_out):
    bsz, L, _ = u.shape
    proj = u @ w_in
    gate, xr, q, k, v = jnp.split(
        proj, [D_RNN, 2 * D_RNN, 2 * D_RNN + Q_DIM, 2 * D_RNN + Q_DIM + KV_DIM], axis=-1)
    xr = causal_dwconv(xr, conv_w, conv_b)
    y_a = jax.nn.gelu(gate, approximate=True) * rg_lru(xr, w_a, b_a, w_x, b_x, lam)
    y_b = swa_sink_alibi(q.reshape(bsz, L, N_Q_HEADS, HEAD_DIM),
                         k.reshape(bsz, L, N_KV_HEADS, HEAD_DIM),
                         v.reshape(bsz, L, N_KV_HEADS, HEAD_DIM), sinks).astype(u.dtype)
    return jnp.concatenate([y_a, y_b], axis=-1) @ w_out


def ssd_chunked(x, dt, a, b_in, c_in):
    bsz, Lp = x.shape[:2]
    nc = Lp // BLOCK
    x = x.reshape(bsz, nc, BLOCK, SSD_GROUPS, SSD_HPG, SSD_HEADDIM)
    dt = dt.reshape(bsz, nc, BLOCK, SSD_GROUPS, SSD_HPG)
    bc = b_in.reshape(bsz, nc, BLOCK, SSD_GROUPS, SSD_STATE)
    cc = c_in.reshape(bsz, nc, BLOCK, SSD_GROUPS, SSD_STATE)
    cs = jnp.cumsum(dt * a.reshape(SSD_GROUPS, SSD_HPG), axis=2)
    xdt = x * dt[..., None]
    cs_t = jnp.moveaxis(cs, 2, -1)
    seg = cs_t[..., :, None] - cs_t[..., None, :]
    tril = jnp.tril(jnp.ones((BLOCK, BLOCK), dtype=bool))
    decay_in = jnp.exp(jnp.where(tril, seg, -jnp.inf))
    cb = jnp.einsum("bclgn,bcsgn->bcgls", cc, bc)
    y_diag = jnp.einsum("bcgls,bcghls,bcsghp->bclghp", cb, decay_in, xdt)
    cs_last = cs[:, :, -1:]
    chunk_states = jnp.einsum("bclgn,bclgh,bclghp->bcghpn", bc, jnp.exp(cs_last - cs), xdt)
    chunk_decay = jnp.exp(cs_last[:, :, 0])

    def step(state, inp):
        dec, st = inp
        return state * dec[..., None, None] + st, state

    init = jnp.zeros_like(chunk_states[:, 0])
    _, prev = lax.scan(step, init, (jnp.moveaxis(chunk_decay, 1, 0), jnp.moveaxis(chunk_states, 1, 0)))
    prev = jnp.moveaxis(prev, 0, 1)
    y_off = jnp.einsum("bclgn,bcghpn,bclgh->bclghp", cc, prev, jnp.exp(cs))
    return (y_diag + y_off).reshape(bsz, Lp, SSD_HEADS, SSD_HEADDIM)


def mamba2_mixer(u, w_in, conv_w, conv_b, dt_bias, a_log, d_skip, gate_norm, w_out):
    bsz, L, _ = u.shape
    proj = u @ w_in
    z, xbc, dt = jnp.split(proj, [D_SSM, D_SSM + SSD_CONV_DIM], axis=-1)
    xbc = jax.nn.silu(causal_dwconv(xbc, conv_w, conv_b))
    xs, bs, cs = jnp.split(xbc, [D_SSM, D_SSM + SSD_GROUPS * SSD_STATE], axis=-1)
    xs = xs.reshape(bsz, L, SSD_HEADS, SSD_HEADDIM).astype(jnp.float32)
    bs = bs.reshape(bsz, L, SSD_GROUPS, SSD_STATE).astype(jnp.float32)
    cs = cs.reshape(bsz, L, SSD_GROUPS, SSD_STATE).astype(jnp.float32)
    dt = jax.nn.softplus(dt.astype(jnp.float32) + dt_bias.astype(jnp.float32))
    a = -jnp.exp(a_log.astype(jnp.float32))

    def front_pad(t):
        return jnp.pad(t, ((0, 0), (PAD, 0)) + ((0, 0),) * (t.ndim - 2))

    y = ssd_chunked(front_pad(xs), front_pad(dt), a, front_pad(bs), front_pad(cs))[:, PAD:]
    y = y + d_skip.astype(jnp.float32)[:, None] * xs
    y = y.reshape(bsz, L, D_SSM) * jax.nn.silu(z.astype(jnp.float32))
    yg = y.reshape(bsz, L, SSD_GROUPS, D_SSM // SSD_GROUPS)
    yg = yg * lax.rsqrt(jnp.mean(yg * yg, axis=-1, keepdims=True) + EPS)
    y = yg.reshape(bsz, L, D_SSM) * gate_norm.astype(jnp.float32)
    return y.astype(u.dtype) @ w_out


def conv_ffn(u, w_up, conv_w, conv_b, w_down):
    h = causal_dwconv(u @ w_up, conv_w, conv_b)
    g, up = jnp.split(h, 2, axis=-1)
    return (jax.nn.gelu(g, approximate=True) * up) @ w_down


def setup_inputs(seed: int = 0) -> dict:
    key = jax.random.key(seed)
    k = jax.random.split(key, 36)

    def normal(kk, shape, scale):
        return jax.random.normal(kk, shape, jnp.float32) * scale

    def gain(kk, n):
        return 1.0 + 0.05 * jax.random.normal(kk, (n,), jnp.float32)

    u = jax.random.uniform(k[11], (D_RNN,), jnp.float32, 0.9, 0.999)
    a0 = u ** (1.0 / LRU_C)
    lru_lambda = jnp.log(a0) - jnp.log1p(-a0)
    dt0 = jnp.exp(jax.random.uniform(k[25], (SSD_HEADS,), jnp.float32, math.log(1e-3), math.log(1e-1)))
    dt_bias = dt0 + jnp.log(-jnp.expm1(-dt0))
    a_log = jnp.log(jax.random.uniform(k[26], (SSD_HEADS,), jnp.float32, 1.0, 16.0))

    return {
        "x": normal(k[0], (BATCH, SEQ, D_MODEL), 1.0),
        "meta_tokens": normal(k[1], (N_META, D_MODEL), 1.0),
        "l0_mix_pre_norm": gain(k[2], D_MODEL),
        "l0_mix_post_norm": gain(k[3], D_MODEL),
        "l0_w_in": normal(k[4], (D_MODEL, EVEN_IN), D_MODEL ** -0.5),
        "l0_lru_conv_w": normal(k[5], (CONV_A, D_RNN), CONV_A ** -0.5),
        "l0_lru_conv_b": normal(k[6], (D_RNN,), 0.01),
        "l0_lru_w_a": normal(k[7], (LRU_BLOCKS, LRU_BS, LRU_BS), LRU_BS ** -0.5),
        "l0_lru_b_a": normal(k[8], (D_RNN,), 0.01),
        "l0_lru_w_x": normal(k[9], (LRU_BLOCKS, LRU_BS, LRU_BS), LRU_BS ** -0.5),
        "l0_lru_b_x": normal(k[10], (D_RNN,), 0.01),
        "l0_lru_lambda": lru_lambda,
        "l0_attn_sinks": normal(k[12], (N_Q_HEADS,), 0.5),
        "l0_w_out": normal(k[13], (EVEN_MIX, D_MODEL), EVEN_MIX ** -0.5),
        "l0_ffn_pre_norm": gain(k[14], D_MODEL),
        "l0_ffn_post_norm": gain(k[15], D_MODEL),
        "l0_ffn_w_up": normal(k[16], (D_MODEL, 2 * D_FF), D_MODEL ** -0.5),
        "l0_ffn_conv_w": normal(k[17], (CONV_F, 2 * D_FF), CONV_F ** -0.5),
        "l0_ffn_conv_b": normal(k[18], (2 * D_FF,), 0.01),
        "l0_ffn_w_down": normal(k[19], (D_FF, D_MODEL), D_FF ** -0.5),
        "l1_mix_pre_norm": gain(k[20], D_MODEL),
        "l1_mix_post_norm": gain(k[21], D_MODEL),
        "l1_w_in": normal(k[22], (D_MODEL, ODD_IN), D_MODEL ** -0.5),
        "l1_ssm_conv_w": normal(k[23], (CONV_C, SSD_CONV_DIM), CONV_C ** -0.5),
        "l1_ssm_conv_b": normal(k[24], (SSD_CONV_DIM,), 0.01),
        "l1_dt_bias": dt_bias,
        "l1_a_log": a_log,
        "l1_d_skip": 1.0 + 0.1 * jax.random.normal(k[27], (SSD_HEADS,), jnp.float32),
        "l1_gate_norm": gain(k[28], D_SSM),
        "l1_w_out": normal(k[29], (D_SSM, D_MODEL), D_SSM ** -0.5),
        "l1_ffn_pre_norm": gain(k[30], D_MODEL),
        "l1_ffn_post_norm": gain(k[31], D_MODEL),
        "l1_ffn_w_up": normal(k[32], (D_MODEL, 2 * D_FF), D_MODEL ** -0.5),
        "l1_ffn_conv_w": normal(k[33], (CONV_F, 2 * D_FF), CONV_F ** -0.5),
        "l1_ffn_conv_b": normal(k[34], (2 * D_FF,), 0.01),
        "l1_ffn_w_down": normal(k[35], (D_FF, D_MODEL), D_FF ** -0.5),
    }


def reference(x, meta_tokens,
              l0_mix_pre_norm, l0_mix_post_norm, l0_w_in, l0_lru_conv_w, l0_lru_conv_b,
              l0_lru_w_a, l0_lru_b_a, l0_lru_w_x, l0_lru_b_x, l0_lru_lambda, l0_attn_sinks, l0_w_out,
              l0_ffn_pre_norm, l0_ffn_post_norm, l0_ffn_w_up, l0_ffn_conv_w, l0_ffn_conv_b, l0_ffn_w_down,
              l1_mix_pre_norm, l1_mix_post_norm, l1_w_in, l1_ssm_conv_w, l1_ssm_conv_b,
              l1_dt_bias, l1_a_log, l1_d_skip, l1_gate_norm, l1_w_out,
              l1_ffn_pre_norm, l1_ffn_post_norm, l1_ffn_w_up, l1_ffn_conv_w, l1_ffn_conv_b, l1_ffn_w_down):
    bsz = x.shape[0]
    meta = jnp.broadcast_to(meta_tokens.astype(x.dtype)[None], (bsz, N_META, D_MODEL))
    h = jnp.concatenate([meta, x], axis=1)
    layers = [
        (l0_mix_pre_norm, l0_mix_post_norm,
         (l0_w_in, l0_lru_conv_w, l0_lru_conv_b, l0_lru_w_a, l0_lru_b_a, l0_lru_w_x, l0_lru_b_x,
          l0_lru_lambda, l0_attn_sinks, l0_w_out),
         (l0_ffn_pre_norm, l0_ffn_post_norm, l0_ffn_w_up, l0_ffn_conv_w, l0_ffn_conv_b, l0_ffn_w_down)),
        (l1_mix_pre_norm, l1_mix_post_norm,
         (l1_w_in, l1_ssm_conv_w, l1_ssm_conv_b, l1_dt_bias, l1_a_log, l1_d_skip, l1_gate_norm, l1_w_out),
         (l1_ffn_pre_norm, l1_ffn_post_norm, l1_ffn_w_up, l1_ffn_conv_w, l1_ffn_conv_b, l1_ffn_w_down)),
    ]
    for i in range(DEPTH):
        pre, post, mix, ffn = layers[i]
        mixer = griffin_swa_mixer if i % 2 == 0 else mamba2_mixer
        h = h + rms_norm(mixer(rms_norm(h, pre), *mix), post)
        f_pre, f_post, w_up, c_w, c_b, w_down = ffn
        h = h + rms_norm(conv_ffn(rms_norm(h, f_pre), w_up, c_w, c_b, w_down), f_post)
    return h[:, N_META:]
```

```python
import numpy as np
import ml_dtypes
import concourse.bass as bass
import concourse.mybir as mybir
from concourse.bass_utils import run_bass_kernel_spmd

F32 = mybir.dt.float32
BF16 = mybir.dt.bfloat16
AF = mybir.ActivationFunctionType
ALU = mybir.AluOpType
AX = mybir.AxisListType

PE, ACT, DVE, POOL, SP = "pe", "act", "dve", "pool", "sp"
ENGS = (PE, ACT, DVE, POOL, SP)


class V:
    def __init__(self, ts, ap):
        self.ts = tuple(ts)
        self.ap = ap

    def __getitem__(self, k):
        return V(self.ts, self.ap[k])

    def r(self, pat, **kw):
        return V(self.ts, self.ap.rearrange(pat, **kw))

    def bc(self, shape):
        return V(self.ts, self.ap.broadcast_to(list(shape)))

    def us(self, axis):
        return V(self.ts, self.ap.unsqueeze(axis))

    def bitcast(self, dt):
        return V(self.ts, self.ap.bitcast(dt))


class T:
    def __init__(self, handle, name):
        self.h = handle
        self.name = name
        self.writers = set()
        self.readers = set()
        self.prev_readers = set()
        self.dma_sem = None
        self.dma_cnt = 0

    def __getitem__(self, k):
        return V((self,), self.h[k])

    @property
    def v(self):
        return V((self,), self.h[:])


def mv(tiles, base_ap):
    return V(tuple(tiles), base_ap)


class Sched:
    def __init__(self, nc):
        self.nc = nc
        self.ops = {e: [] for e in ENGS}
        self.tiles = []

    def track(self, handle, name):
        t = T(handle, name)
        self.tiles.append(t)
        return t

    def sb(self, name, shape, dtype):
        return self.track(self.nc.alloc_sbuf_tensor(name, list(shape), dtype), name)

    def ps(self, name, shape, dtype=F32):
        return self.track(self.nc.alloc_psum_tensor(name, list(shape), dtype), name)

    def op(self, eng, fn, reads=(), writes=(), acc=False, dma_tile=None):
        idx = len(self.ops[eng])
        me = (eng, idx)
        deps = set()
        for t in reads:
            deps |= t.writers
        for t in writes:
            if acc and t.writers:
                deps |= t.prev_readers
            else:
                deps |= t.writers | t.readers
        raw = set()
        for t in reads:
            raw |= t.writers
        fdeps = []
        for d in deps:
            if d[0] == eng:
                if eng == PE:
                    continue
                if eng in (ACT, DVE, POOL) and d not in raw:
                    continue
            fdeps.append(d)
        best = {}
        keep = []
        for d in fdeps:
            if self.ops[d[0]][d[1]][2] is not None:
                keep.append(d)
            elif d[0] not in best or best[d[0]][1] < d[1]:
                best[d[0]] = d
        fdeps = keep + list(best.values())
        self.ops[eng].append([fn, fdeps, dma_tile])
        for t in reads:
            t.readers.add(me)
        for t in writes:
            if acc and t.writers:
                t.writers.add(me)
            else:
                t.prev_readers = t.readers
                t.writers = {me}
                t.readers = set()
        return me

    def dma(self, eng, out_ap, in_ap, reads=(), writes=(), key=None, acc=False, **kw):
        assert key is not None
        return self.op(eng, lambda e: e.dma_start(out=out_ap, in_=in_ap, **kw),
                       reads=reads, writes=writes, acc=acc, dma_tile=key)

    @staticmethod
    def _ts(*vs):
        out = []
        for v in vs:
            if isinstance(v, V):
                out.extend(v.ts)
        return out

    @staticmethod
    def _a(v):
        return v.ap if isinstance(v, V) else v

    def act(self, out, in_, func, bias=0.0, scale=1.0, accum=None, acc=False):
        a = self._a
        kw = {}
        if accum is not None:
            kw["accum_out"] = a(accum)
        return self.op(ACT, lambda e: e.activation(out=a(out), in_=a(in_), func=func, bias=a(bias), scale=a(scale), **kw),
                       reads=self._ts(in_, bias, scale), writes=self._ts(out, accum), acc=acc)

    def tt(self, out, in0, in1, op, eng=DVE, acc=False):
        a = self._a
        return self.op(eng, lambda e: e.tensor_tensor(out=a(out), in0=a(in0), in1=a(in1), op=op),
                       reads=self._ts(in0, in1), writes=self._ts(out), acc=acc)

    def ts(self, out, in0, s1, op0, s2=None, op1=None, eng=DVE, acc=False):
        a = self._a
        if op1 is None:
            f = lambda e: e.tensor_scalar(out=a(out), in0=a(in0), scalar1=a(s1), scalar2=None, op0=op0)
        else:
            f = lambda e: e.tensor_scalar(out=a(out), in0=a(in0), scalar1=a(s1), scalar2=a(s2), op0=op0, op1=op1)
        return self.op(eng, f, reads=self._ts(in0, s1, s2), writes=self._ts(out), acc=acc)

    def stt(self, out, in0, scalar, in1, op0, op1, acc=False):
        a = self._a
        return self.op(DVE, lambda e: e.scalar_tensor_tensor(out=a(out), in0=a(in0), scalar=a(scalar), in1=a(in1), op0=op0, op1=op1),
                       reads=self._ts(in0, scalar, in1), writes=self._ts(out), acc=acc)

    def scan(self, out, d0, d1, init):
        a = self._a
        return self.op(DVE, lambda e: e.tensor_tensor_scan(out=a(out), data0=a(d0), data1=a(d1), initial=a(init), op0=ALU.mult, op1=ALU.add),
                       reads=self._ts(d0, d1, init), writes=self._ts(out))

    def copy(self, out, in_, eng=DVE, acc=False):
        a = self._a
        if eng == ACT:
            f = lambda e: e.copy(out=a(out), in_=a(in_))
        else:
            f = lambda e: e.tensor_copy(out=a(out), in_=a(in_))
        return self.op(eng, f, reads=self._ts(in_), writes=self._ts(out), acc=acc)

    def recip(self, out, in_):
        a = self._a
        return self.op(DVE, lambda e: e.reciprocal(out=a(out), in_=a(in_)), reads=self._ts(in_), writes=self._ts(out))

    def memset(self, out, val, eng=POOL, acc=False):
        a = self._a
        return self.op(eng, lambda e: e.memset(a(out), val), writes=self._ts(out), acc=acc)

    def mm(self, out, lhsT, rhs, start=True, stop=True, acc=None):
        a = self._a
        if acc is None:
            acc = not start
        return self.op(PE, lambda e: e.matmul(a(out), lhsT=a(lhsT), rhs=a(rhs), start=start, stop=stop),
                       reads=self._ts(lhsT, rhs), writes=self._ts(out), acc=acc)

    def tr(self, out, in_, ident, acc=False):
        a = self._a
        return self.op(PE, lambda e: e.transpose(a(out), a(in_), a(ident)),
                       reads=self._ts(in_, ident), writes=self._ts(out), acc=acc)

    def dmav(self, eng, out, in_, key, acc=False):
        a = self._a
        return self.op(eng, lambda e: e.dma_start(out=a(out), in_=a(in_)),
                       reads=self._ts(in_), writes=self._ts(out), acc=acc, dma_tile=key)

    def emit(self):
        nc = self.nc
        needed = set()
        for e in ENGS:
            for fn, deps, dt_ in self.ops[e]:
                for d in deps:
                    needed.add(d)
        eng_sem = {e: nc.alloc_semaphore("sem_" + e) for e in (PE, ACT, DVE, POOL)}
        sig = {}
        for e in (PE, ACT, DVE, POOL, SP):
            cnt = 0
            for i, (fn, deps, dt_) in enumerate(self.ops[e]):
                if dt_ is not None:
                    if dt_.dma_sem is None:
                        dt_.dma_sem = nc.alloc_semaphore("dsem_" + dt_.name)
                    dt_.dma_cnt += 16
                    sig[(e, i)] = ("d_" + dt_.name, dt_.dma_sem, dt_.dma_cnt, 16)
                elif (e, i) in needed:
                    assert e != SP
                    cnt += 1
                    sig[(e, i)] = ("e_" + e, eng_sem[e], cnt, 1)
        self.sem_max = {}
        for k_, s_, v_, i_ in sig.values():
            self.sem_max[k_] = max(self.sem_max.get(k_, 0), v_)
        self.n_ops = {e: len(self.ops[e]) for e in ENGS}
        with nc.Block() as block:
            def body(e):
                def run(eng):
                    seen = {}
                    for i, (fn, deps, dt_) in enumerate(self.ops[e]):
                        want = {}
                        for d in deps:
                            k, s, v, _ = sig[d]
                            if v > want.get(k, (0, None))[0]:
                                want[k] = (v, s)
                        for k, (v, s) in want.items():
                            if seen.get(k, 0) < v:
                                eng.wait_ge(s, v)
                                seen[k] = v
                        ins = fn(eng)
                        if (e, i) in sig and ins is not None:
                            k, s, v, inc = sig[(e, i)]
                            ins.then_inc(s, inc)
                return run
            block.tensor(body(PE))
            block.scalar(body(ACT))
            block.vector(body(DVE))
            block.gpsimd(body(POOL))
            block.sync(body(SP))


D = 1024
TT = 512
N_META = 16
SEQ = 8192
NEG = -30000.0
EPS = 1e-6
NQH = 16

PP_SPEC = [
    ("n0pre", 8), ("n0post", 8), ("f0pre", 8), ("f0post", 8),
    ("n1pre", 8), ("n1post", 8), ("f1pre", 8), ("f1post", 8),
    ("lru_cw", 32), ("lru_cb", 8), ("b_a", 8), ("b_x", 8), ("lam", 8),
    ("f0_cw", 132), ("f0_cb", 44), ("f1_cw", 132), ("f1_cb", 44),
    ("ssm_cw", 128), ("ssm_cb", 32), ("gnorm", 16), ("dskip", 16),
    ("sinks", 16), ("dt_bias", 32), ("a_log", 32),
]
CC_SPEC = [("ident", 128), ("tri", 128), ("stri", 128), ("abd", 2048), ("abp", 2048),
           ("Dm", 256), ("maskm", 256), ("nslope", 16), ("nb128", 16)]


def _offsets(spec):
    off, o = {}, 0
    for n, c in spec:
        off[n] = o
        o += c
    return off, o


PP_OFF, NPP = _offsets(PP_SPEC)
CC_OFF, NCC = _offsets(CC_SPEC)


def _chunk(v):
    v = np.asarray(v, np.float32)
    return np.ascontiguousarray(v.reshape(-1, 128).T)


def _row(v):
    v = np.asarray(v, np.float32)
    return np.ascontiguousarray(np.broadcast_to(v[None, :], (128, v.shape[0])))


def pack_params(inp):
    pp = np.zeros((128, NPP), np.float32)

    def put(name, arr):
        pp[:, PP_OFF[name]:PP_OFF[name] + arr.shape[1]] = arr
    put("n0pre", _chunk(inp["l0_mix_pre_norm"])); put("n0post", _chunk(inp["l0_mix_post_norm"]))
    put("f0pre", _chunk(inp["l0_ffn_pre_norm"])); put("f0post", _chunk(inp["l0_ffn_post_norm"]))
    put("n1pre", _chunk(inp["l1_mix_pre_norm"])); put("n1post", _chunk(inp["l1_mix_post_norm"]))
    put("f1pre", _chunk(inp["l1_ffn_pre_norm"])); put("f1post", _chunk(inp["l1_ffn_post_norm"]))
    put("lru_cw", np.concatenate([_chunk(inp["l0_lru_conv_w"][k]) for k in range(4)], axis=1))
    put("lru_cb", _chunk(inp["l0_lru_conv_b"]))
    put("b_a", _chunk(inp["l0_lru_b_a"])); put("b_x", _chunk(inp["l0_lru_b_x"])); put("lam", _chunk(inp["l0_lru_lambda"]))
    for l in (0, 1):
        put(f"f{l}_cw", np.concatenate([_chunk(inp[f"l{l}_ffn_conv_w"][k]) for k in range(3)], axis=1))
        put(f"f{l}_cb", _chunk(inp[f"l{l}_ffn_conv_b"]))
    put("ssm_cw", np.concatenate([_chunk(inp["l1_ssm_conv_w"][k]) for k in range(4)], axis=1))
    put("ssm_cb", _chunk(inp["l1_ssm_conv_b"]))
    put("gnorm", _chunk(inp["l1_gate_norm"]))
    put("dskip", _chunk(np.repeat(np.asarray(inp["l1_d_skip"], np.float32), 64)))
    put("sinks", _row(inp["l0_attn_sinks"])); put("dt_bias", _row(inp["l1_dt_bias"])); put("a_log", _row(inp["l1_a_log"]))
    return pp


def make_consts():
    cc = np.zeros((128, NCC), np.float32)

    def put(name, arr):
        cc[:, CC_OFF[name]:CC_OFF[name] + arr.shape[1]] = arr
    idx = np.arange(128)
    put("ident", np.eye(128, dtype=np.float32))
    put("tri", (idx[:, None] <= idx[None, :]).astype(np.float32))
    put("stri", (idx[None, :] < idx[:, None]).astype(np.float32))
    slopes = (2.0 ** (-8.0 * np.arange(1, NQH + 1, dtype=np.float32) / NQH)).astype(np.float32)
    tk = idx[:, None].astype(np.float32)
    tq = idx[None, :].astype(np.float32)
    dd = tq - tk
    dp = tq + 128.0 - tk
    abd = np.zeros((128, NQH, 128), np.float32)
    abp = np.zeros((128, NQH, 128), np.float32)
    for h in range(NQH):
        abd[:, h, :] = np.where(dd >= 0, -slopes[h] * dd, NEG)
        abp[:, h, :] = np.where(dp < 128, -slopes[h] * dp, NEG)
    put("abd", abd.reshape(128, -1)); put("abp", abp.reshape(128, -1))
    t = np.arange(256)[None, :].astype(np.float32)
    s_ = idx[:, None].astype(np.float32)
    dm = np.minimum(t - s_, 128.0)
    ok = (t - s_) >= 0
    put("Dm", np.where(ok, dm, 0.0).astype(np.float32))
    put("maskm", np.where(ok, 0.0, NEG).astype(np.float32))
    put("nslope", _row(-slopes)); put("nb128", _row(-128.0 * slopes))
    return cc


class Rot:
    def __init__(self, items):
        self.items = list(items)
        self.i = 0

    def get(self):
        t = self.items[self.i % len(self.items)]
        self.i += 1
        return t


def build(NT, stages=4, nslot=5):
    nc = bass.Bass("TRN2", target_bir_lowering=False)
    S = Sched(nc)
    Tn = NT * TT
    dram = lambda n, sh, kind="ExternalInput": nc.dram_tensor(n, list(sh), F32, kind=kind).ap()
    xT_d = dram("xT", [D, Tn])
    out_d = dram("outT", [D, Tn], "ExternalOutput")
    pp_d = dram("pp", [128, NPP])
    cc_d = dram("cc", [128, NCC])
    W = {n: dram(n, sh) for n, sh in [
        ("l0_w_in", [1024, 3328]), ("l0_w_out", [2048, 1024]), ("l0_lru_w_a", [8, 128, 128]), ("l0_lru_w_x", [8, 128, 128]),
        ("l0_ffn_w_up", [1024, 5632]), ("l0_ffn_w_down", [2816, 1024]),
        ("l1_w_in", [1024, 6176]), ("l1_w_out", [2048, 1024]), ("l1_ffn_w_up", [1024, 5632]), ("l1_ffn_w_down", [2816, 1024])]}
    outT = S.track(out_d, "outT_dram")

    pp = S.sb("pp_s", [128, NPP], F32)
    cc = S.sb("cc_s", [128, NCC], F32)
    P = lambda n, c=0, w=1: pp[:, PP_OFF[n] + c:PP_OFF[n] + c + w]
    C = lambda n, a=0, b=None: cc[:, CC_OFF[n] + a:CC_OFF[n] + (dict(CC_SPEC)[n] if b is None else b)]
    drv = S.sb("drv", [128, 8 + 32 + 16], F32)
    m8sp = lambda c: drv[:, c:c + 1]
    Aneg = drv[:, 8:40]
    esink = lambda h: drv[:, 40 + h:41 + h]
    ones_bf = S.sb("ones_bf", [128, 128], BF16)
    ones_f = S.sb("ones_f", [128, 128], F32)
    ident_bf = S.sb("ident_bf", [128, 128], BF16)
    wa_bf = S.sb("wa_bf", [128, 8, 128], BF16)
    wx_bf = S.sb("wx_bf", [128, 8, 128], BF16)
    wdt_bf = S.sb("wdt_bf", [128, 8, 32], BF16)
    h = [S.sb(f"h{c}", [128, TT], F32) for c in range(8)]
    xn = [S.sb(f"xn{c}", [128, TT], BF16) for c in range(8)]
    NBFA, NBFT, NFT = 46, 6, 14
    bfa_h = nc.alloc_sbuf_tensor("bfa", [128, NBFA * TT], BF16)
    BFA = [S.track(bfa_h[:, i * TT:(i + 1) * TT], f"bfa{i}") for i in range(NBFA)]
    bfm = lambda i, n: mv(BFA[i:i + n], bfa_h[:, i * TT:(i + n) * TT])
    bft = Rot([S.sb(f"bft{i}", [128, TT], BF16) for i in range(NBFT)])
    ft = Rot([S.sb(f"ft{i}", [128, TT], F32) for i in range(NFT)])
    xcb = Rot([S.sb(f"xcb{i}", [128, TT + 4], F32) for i in range(3)])
    ring = [S.sb(f"ring{i}", [128, 8, TT], BF16) for i in range(nslot)]
    PSB = [S.ps(f"psb{i}", [128, TT]) for i in range(8)]
    psr = Rot(PSB[:7])
    pstat = PSB[7]
    sml = Rot([S.sb(f"sml{i}", [128, 128], F32) for i in range(8)])
    lruh_h = nc.alloc_sbuf_tensor("lruh", [128, 8 * 3], F32)
    lruh = [S.track(lruh_h[:, 3 * c:3 * c + 3], f"lruh{c}") for c in range(8)]
    lrus_h = nc.alloc_sbuf_tensor("lrus", [128, 8], F32)
    lrus = [S.track(lrus_h[:, c:c + 1], f"lrus{c}") for c in range(8)]
    kbuf = [S.sb(f"kbuf{g}", [128, 128 + TT], BF16) for g in range(2)]
    vbuf = S.sb("vbuf", [128, 5, 128], BF16)
    kmeta = [S.sb(f"kmeta{g}", [128, 16], BF16) for g in range(2)]
    vmeta = S.sb("vmeta", [128, 128], BF16)
    fh_h = [nc.alloc_sbuf_tensor(f"fh{l}", [128, 44 * 2], F32) for l in range(2)]
    fh = [[S.track(fh_h[l][:, 2 * c:2 * c + 2], f"fh{l}_{c}") for c in range(44)] for l in range(2)]
    sh_h = nc.alloc_sbuf_tensor("sh", [128, 32 * 3], F32)
    sh = [S.track(sh_h[:, 3 * c:3 * c + 3], f"sh{c}") for c in range(32)]
    st_h = nc.alloc_sbuf_tensor("state", [128, 2048], F32)
    state = [S.track(st_h[:, 256 * g:256 * g + 256], f"state{g}") for g in range(8)]
    stb_h = nc.alloc_sbuf_tensor("stbf", [128, 2048], BF16)
    stbf = [S.track(stb_h[:, 256 * g:256 * g + 256], f"stbf{g}") for g in range(8)]
    dts = S.sb("dts", [128, 4, 32], F32)
    dtA = S.sb("dtA", [128, 4, 32], F32)

    S.dmav(SP, pp.v, pp_d[:, :], key=pp)
    S.dmav(SP, cc.v, cc_d[:, :], key=cc)
    S.dmav(POOL, wa_bf.v, W["l0_lru_w_a"].rearrange("n i j -> i n j"), key=wa_bf)
    S.dmav(POOL, wx_bf.v, W["l0_lru_w_x"].rearrange("n i j -> i n j"), key=wx_bf)
    S.dmav(POOL, wdt_bf.v, W["l1_w_in"][:, 6144:6176].rearrange("(k p) n -> p k n", p=128), key=wdt_bf)
    S.memset(ones_bf.v, 1.0)
    S.memset(ones_f.v, 1.0)
    S.copy(ident_bf.v, C("ident"))
    for t_ in lruh + lrus + sh + state + stbf + [x for l in fh for x in l] + kbuf + kmeta + [vbuf, vmeta]:
        S.memset(t_.v, 0.0)
    t0_ = sml.get()
    S.act(t0_[:, 0:8], P("lam", 0, 8), AF.Exp, scale=-1.0)
    S.act(t0_[:, 8:16], t0_[:, 0:8], AF.Ln, bias=1.0)
    S.ts(drv[:, 0:8], t0_[:, 8:16], -8.0, ALU.mult)
    S.act(t0_[:, 16:48], P("a_log", 0, 32), AF.Exp)
    S.ts(drv[:, 8:40], t0_[:, 16:48], -1.0, ALU.mult)
    S.act(drv[:, 40:56], P("sinks", 0, 16), AF.Exp)

    ucount = [0]

    def load_unit(wname, r0, nk, c0, ncol, rows_per_k=128):
        slot = ring[ucount[0] % nslot]
        ucount[0] += 1
        src = W[wname]
        first = True
        for k0 in range(0, nk, 4):
            k1 = min(nk, k0 + 4)
            S.dmav(POOL, slot[:, k0:k1, 0:ncol],
                   src[r0 + k0 * 128:r0 + k1 * 128, c0:c0 + ncol].rearrange("(k p) n -> p k n", p=128),
                   key=slot, acc=not first)
            first = False
        return slot

    def rmsnorm_to_xn(wname):
        for c in range(8):
            sq = bft.get()
            S.act(sq.v, h[c].v, AF.Square)
            S.mm(pstat.v, ones_bf.v, sq.v, start=(c == 0), stop=(c == 7))
        rs = ft.get()
        S.act(rs.v, pstat.v, AF.Sqrt, bias=EPS_AP(), scale=1.0 / D)
        S.recip(rs.v, rs.v)
        for c in range(8):
            S.stt(xn[c].v, h[c].v, P(wname, c), rs.v, ALU.mult, ALU.mult)

    eps_t = S.sb("eps_t", [128, 1], F32)
    S.memset(eps_t.v, EPS)
    EPS_AP = lambda: eps_t[:, 0:1]

    class Epi:
        def __init__(self, wname):
            self.w = wname
            self.mo = []

        def chunk(self, oc, ps):
            mo = ft.get()
            S.copy(mo.v, ps.v, eng=ACT)
            sq = bft.get()
            S.act(sq.v, ps.v, AF.Square)
            S.mm(pstat.v, ones_bf.v, sq.v, start=(oc == 0), stop=(oc == 7))
            self.mo.append(mo)

        def finish(self):
            rs = ft.get()
            S.act(rs.v, pstat.v, AF.Sqrt, bias=EPS_AP(), scale=1.0 / D)
            S.recip(rs.v, rs.v)
            for c in range(8):
                S.stt(self.mo[c].v, self.mo[c].v, P(self.w, c), rs.v, ALU.mult, ALU.mult)
                S.tt(h[c].v, h[c].v, self.mo[c].v, ALU.add)

    def conv_chunk(ps, halo, K, wname, bname, cidx, nch):
        xb = xcb.get()
        S.copy(xb[:, 0:K - 1], halo.v)
        S.copy(xb[:, K - 1:K - 1 + TT], ps.v, eng=ACT)
        S.copy(halo.v, xb[:, TT:TT + K - 1], eng=POOL)
        cv = ft.get()
        wcol = lambda k: P(wname, k * nch + cidx)
        S.ts(cv.v, xb[:, K - 1:K - 1 + TT], wcol(K - 1), ALU.mult, P(bname, cidx), ALU.add)
        for k in range(K - 2, -1, -1):
            S.stt(cv.v, xb[:, k:k + TT], wcol(k), cv.v, ALU.mult, ALU.add)
        return cv

    def proj_chunk(slot_of_k, j, rhs_list, ncolj=128):
        ps = psr.get()
        n = len(rhs_list)
        for k in range(n):
            sl, kk = slot_of_k(k)
            S.mm(ps[0:ncolj, :], sl[:, kk, j * 128:j * 128 + ncolj], rhs_list[k].v, start=(k == 0), stop=(k == n - 1))
        return ps

    def l0_mixer(ti):
        rmsnorm_to_xn("n0pre")
        gate, q, ya, yb = BFA[0:8], BFA[8:16], BFA[16:24], BFA[24:32]
        for u in range(7):
            ncol = 512 if u < 6 else 256
            slot = load_unit("l0_w_in", 0, 8, 512 * u, ncol)
            sk = lambda k, slot=slot: (slot, k)
            if u < 6:
                for j in range(4):
                    c = (u % 2) * 4 + j
                    ps = proj_chunk(sk, j, xn)
                    if u < 2:
                        S.act(gate[c].v, ps.v, AF.Gelu_apprx_tanh)
                    elif u < 4:
                        lru_chunk(c, ps, gate[c], ya[c])
                    else:
                        S.copy(q[c].v, ps.v, eng=ACT)
            else:
                for g in range(2):
                    ps = psr.get()
                    for half in range(2):
                        for k in range(8):
                            S.mm(ps[64 * half:64 * half + 64, :], slot[:, k, 64 * g:64 * g + 64], xn[k].v,
                                 start=(k == 0), stop=(k == 7), acc=not (half == 0 and k == 0))
                    S.copy(kbuf[g][:, 128:128 + TT], ps.v, eng=ACT)
                    if ti == 0:
                        S.copy(kmeta[g].v, kbuf[g][:, 128:144])
                ps = psr.get()
                for blk in range(4):
                    for k in range(8):
                        S.mm(ps[:, blk * 128:blk * 128 + 128], xn[k][:, blk * 128:blk * 128 + 128], slot[:, k, 128:256],
                             start=(k == 0), stop=(k == 7), acc=not (blk == 0 and k == 0))
                S.copy(vbuf[:, 1:5, :], ps.v.r("p (b n) -> p b n", b=4), eng=ACT)
                if ti == 0:
                    S.copy(vmeta[0:16, :], vbuf[0:16, 1, :])
        attention(ti, q, yb)
        for g in range(2):
            S.copy(kbuf[g][:, 0:128], kbuf[g][:, TT:TT + 128], eng=POOL)
        S.copy(vbuf[:, 0, :], vbuf[:, 4, :], eng=POOL)
        epi = Epi("n0post")
        for cg in range(2):
            ua = load_unit("l0_w_out", 0, 8, 512 * cg, 512)
            ub = load_unit("l0_w_out", 1024, 8, 512 * cg, 512)
            sk = lambda k, ua=ua, ub=ub: (ua, k) if k < 8 else (ub, k - 8)
            for j in range(4):
                ps = proj_chunk(sk, j, list(ya) + list(yb))
                epi.chunk(4 * cg + j, ps)
        epi.finish()

    def lru_chunk(c, ps, gate_c, ya_c):
        cv = conv_chunk(ps, lruh[c], 4, "lru_cw", "lru_cb", c, 8)
        xvb = bft.get()
        S.copy(xvb.v, cv.v, eng=ACT)
        pr, pi = psr.get(), psr.get()
        S.mm(pr.v, wa_bf[:, c, :], xvb.v)
        S.mm(pi.v, wx_bf[:, c, :], xvb.v)
        r, i_, a, om = ft.get(), ft.get(), ft.get(), ft.get()
        S.act(r.v, pr.v, AF.Sigmoid, bias=P("b_a", c))
        S.act(i_.v, pi.v, AF.Sigmoid, bias=P("b_x", c))
        S.act(a.v, r.v, AF.Exp, scale=m8sp(c))
        S.tt(om.v, a.v, a.v, ALU.mult)
        S.ts(om.v, om.v, -1.0, ALU.mult, 1.0, ALU.add)
        S.act(om.v, om.v, AF.Sqrt)
        S.tt(i_.v, i_.v, cv.v, ALU.mult)
        S.tt(i_.v, i_.v, om.v, ALU.mult)
        hs = ft.get()
        S.scan(hs.v, a.v, i_.v, lrus[c].v)
        S.copy(lrus[c].v, hs[:, TT - 1:TT], eng=POOL)
        S.tt(ya_c.v, gate_c.v, hs.v, ALU.mult)

    def attention(ti, q, yb):
        for hp in range(8):
            g = hp // 4
            for blk in range(4):
                n = 4 * ti + blk
                ps, pm, pb = psr.get(), psr.get(), psr.get()
                for e in range(2):
                    b0 = 64 * e
                    qh = q[hp][b0:b0 + 64, blk * 128:blk * 128 + 128]
                    first = (e == 0)
                    S.mm(ps[:, e * 128:e * 128 + 128], kbuf[g][b0:b0 + 64, 128 + blk * 128:256 + blk * 128], qh, acc=not first)
                    S.mm(ps[:, 256 + e * 128:384 + e * 128], kbuf[g][b0:b0 + 64, blk * 128:blk * 128 + 128], qh, acc=True)
                    S.mm(pm[0:16, e * 128:e * 128 + 128], kmeta[g][b0:b0 + 64, 0:16], qh, acc=not first)
                sb = ft.get()
                S.stt(sb[:, 0:256], ps[:, 0:256], 0.125, C("abd", 256 * hp, 256 * hp + 256), ALU.mult, ALU.add)
                S.stt(sb[:, 256:512], ps[:, 256:512], 0.125, C("abp", 256 * hp, 256 * hp + 256), ALU.mult, ALU.add, acc=True)
                pT = bft.get()
                S.act(pT.v, sb.v, AF.Exp)
                if n == 0:
                    S.memset(pT[:, 256:512], 0.0, eng=DVE)
                    S.memset(pT[0:16, 0:256], 0.0, eng=DVE)
                elif n == 1:
                    S.memset(pT[0:16, 256:512], 0.0, eng=DVE)
                pTm = bft.get()
                for e in range(2):
                    hh = 2 * hp + e
                    dstm = pTm[0:16, e * 128:e * 128 + 128]
                    srcm = pm[0:16, e * 128:e * 128 + 128]
                    if n >= 2:
                        S.act(dstm, srcm, AF.Exp, bias=C("nb128", hh, hh + 1)[0:16, :], scale=0.125, acc=(e == 1))
                    else:
                        tm = sml.get()
                        S.stt(tm[0:16, :], C("Dm", n * 128, n * 128 + 128)[0:16, :], C("nslope", hh, hh + 1)[0:16, :],
                              C("maskm", n * 128, n * 128 + 128)[0:16, :], ALU.mult, ALU.add)
                        S.stt(tm[0:16, :], srcm, 0.125, tm[0:16, :], ALU.mult, ALU.add)
                        S.act(dstm, tm[0:16, :], AF.Exp, acc=(e == 1))
                for e in range(2):
                    b0 = 64 * e
                    for o0 in (0, 256):
                        dst = pb[b0:b0 + 64, o0 + e * 128:o0 + e * 128 + 128]
                        if o0 == 0:
                            l1, l2, l3 = vbuf[:, 1 + blk, 64 * g:64 * g + 64], vbuf[:, blk, 64 * g:64 * g + 64], vmeta[0:16, 64 * g:64 * g + 64]
                        else:
                            l1, l2, l3 = ones_bf[:, 0:64], ones_bf[:, 0:64], ones_bf[0:16, 0:64]
                        S.mm(dst, l1, pT[:, e * 128:e * 128 + 128], start=True, stop=False, acc=not (e == 0 and o0 == 0))
                        S.mm(dst, l2, pT[:, 256 + e * 128:384 + e * 128], start=False, stop=False)
                        S.mm(dst, l3, pTm[0:16, e * 128:e * 128 + 128], start=False, stop=True)
                dn = sml.get()
                for e in range(2):
                    b0 = 64 * e
                    S.ts(dn[b0:b0 + 64, :], pb[b0:b0 + 64, 256 + e * 128:384 + e * 128], esink(2 * hp + e)[b0:b0 + 64, :], ALU.add, acc=(e == 1))
                S.recip(dn.v, dn.v)
                for e in range(2):
                    b0 = 64 * e
                    S.tt(yb[hp][b0:b0 + 64, blk * 128:blk * 128 + 128], pb[b0:b0 + 64, e * 128:e * 128 + 128], dn[b0:b0 + 64, :], ALU.mult, acc=True)

    def ffn(l):
        rmsnorm_to_xn(f"f{l}pre")
        actb = BFA[0:22]
        for u in range(11):
            slot = load_unit(f"l{l}_ffn_w_up", 0, 8, 512 * u, 512)
            sk = lambda k, slot=slot: (slot, k)
            for j in range(4):
                ci = 4 * u + j
                ps = proj_chunk(sk, j, xn)
                cv = conv_chunk(ps, fh[l][ci], 3, f"f{l}_cw", f"f{l}_cb", ci, 44)
                if ci < 22:
                    S.act(actb[ci].v, cv.v, AF.Gelu_apprx_tanh)
                else:
                    S.tt(actb[ci - 22].v, actb[ci - 22].v, cv.v, ALU.mult)
        epi = Epi(f"f{l}post")
        for cg in range(2):
            us = [load_unit(f"l{l}_ffn_w_down", 1024 * i, 8 if i < 2 else 6, 512 * cg, 512) for i in range(3)]
            sk = lambda k, us=us: (us[k // 8], k % 8)
            for j in range(4):
                ps = proj_chunk(sk, j, actb)
                epi.chunk(4 * cg + j, ps)
        epi.finish()

    def l1_mixer(ti):
        rmsnorm_to_xn("n1pre")
        xf, Bf, Cf = BFA[0:16], BFA[16:24], BFA[24:32]
        xdt_v, xdtd_v, Btok_v, ytok_v = bfm(32, 4), bfm(36, 4), bfm(40, 2), bfm(42, 4)
        tri, stri = C("tri"), C("stri")
        for blk in range(4):
            ps = psr.get()
            for k in range(8):
                S.mm(ps[:, 0:32], xn[k][:, blk * 128:blk * 128 + 128], wdt_bf[:, k, :], start=(k == 0), stop=(k == 7))
            t1 = sml.get()
            S.tt(t1[:, 0:32], ps[:, 0:32], P("dt_bias", 0, 32), ALU.add)
            S.act(t1[:, 32:64], t1[:, 0:32], AF.Exp)
            S.act(dts[:, blk, :], t1[:, 32:64], AF.Ln, bias=1.0, acc=(blk > 0))
            S.tt(dtA[:, blk, :], dts[:, blk, :], Aneg, ALU.mult, acc=(blk > 0))
        for u in range(8):
            slot = load_unit("l1_w_in", 0, 8, 2048 + 512 * u, 512)
            sk = lambda k, slot=slot: (slot, k)
            for j in range(4):
                cc_ = 4 * u + j
                ps = proj_chunk(sk, j, xn)
                cv = conv_chunk(ps, sh[cc_], 4, "ssm_cw", "ssm_cb", cc_, 32)
                dst = xf[cc_] if cc_ < 16 else (Bf[cc_ - 16] if cc_ < 24 else Cf[cc_ - 24])
                S.act(dst.v, cv.v, AF.Silu)
        for blk in range(4):
            ssd_chunk(blk, xf, Bf, Cf, xdt_v, xdtd_v, Btok_v, ytok_v, tri, stri)
        for u in range(4):
            slot = load_unit("l1_w_in", 0, 8, 512 * u, 512)
            sk = lambda k, slot=slot: (slot, k)
            for j in range(4):
                c = 4 * u + j
                ps = proj_chunk(sk, j, xn)
                sz = ft.get()
                S.act(sz.v, ps.v, AF.Silu)
                S.tt(xf[c].v, xf[c].v, sz.v, ALU.mult)
                if c % 2 == 1:
                    pg = psr.get()
                    for i2, c2 in enumerate((c - 1, c)):
                        sq = bft.get()
                        S.act(sq.v, xf[c2].v, AF.Square)
                        S.mm(pg.v, ones_bf.v, sq.v, start=(i2 == 0), stop=(i2 == 1))
                    rs = ft.get()
                    S.act(rs.v, pg.v, AF.Sqrt, bias=EPS_AP(), scale=1.0 / 256)
                    S.recip(rs.v, rs.v)
                    for c2 in (c - 1, c):
                        S.stt(xf[c2].v, xf[c2].v, P("gnorm", c2), rs.v, ALU.mult, ALU.mult)
        epi = Epi("n1post")
        for cg in range(2):
            us = [load_unit("l1_w_out", 1024 * i, 8, 512 * cg, 512) for i in range(2)]
            sk = lambda k, us=us: (us[k // 8], k % 8)
            for j in range(4):
                ps = proj_chunk(sk, j, xf)
                epi.chunk(4 * cg + j, ps)
        epi.finish()

    def ssd_chunk(blk, xf, Bf, Cf, xdt_v, xdtd_v, Btok_v, ytok_v, tri, stri):
        cs_ = slice(blk * 128, blk * 128 + 128)
        dt_b = dts[:, blk, :]
        dtA_b = dtA[:, blk, :]
        for half in range(2):
            pt = psr.get()
            ptb = pt.v.bitcast(BF16)
            for c in range(8):
                S.tr(ptb[:, c * 128:c * 128 + 128], xf[8 * half + c][:, cs_], ident_bf.v, acc=(c > 0))
            S.tt(xdt_v[:, half * 1024:half * 1024 + 1024].r("p (h d) -> p h d", d=64), ptb.r("p (h d) -> p h d", d=64),
                 dt_b[:, 16 * half:16 * half + 16].us(2).bc([128, 16, 64]), ALU.mult, acc=(half == 1))
        pt = psr.get()
        ptb = pt.v.bitcast(BF16)
        for g in range(8):
            S.tr(ptb[:, g * 128:g * 128 + 128], Bf[g][:, cs_], ident_bf.v, acc=(g > 0))
        S.copy(Btok_v, ptb, eng=ACT)
        pc = psr.get()
        S.mm(pc[:, 0:32], tri, dtA_b)
        S.mm(pc[:, 32:64], ones_f.v, dtA_b, acc=True)
        sm = sml.get()
        S.act(sm[:, 0:64], pc[:, 0:64], AF.Exp)
        ecs, cdec = sm[:, 0:32], sm[:, 32:64]
        d2e = sml.get()
        ycols = ytok_v
        for half in range(2):
            pcb = psr.get()
            for gg in range(4):
                g = 4 * half + gg
                S.mm(pcb[:, gg * 128:gg * 128 + 128], Bf[g][:, cs_], Cf[g][:, cs_], acc=(gg > 0))
            cbm = ft.get()
            S.tt(cbm.v.r("p (g l) -> p g l", g=4), pcb.v.r("p (g l) -> p g l", g=4), tri.us(1).bc([128, 4, 128]), ALU.mult)
            for gg in range(4):
                g = 4 * half + gg
                lt = ft.get()
                S.tt(lt.v.r("p (h s) -> p h s", h=4), stri.us(1).bc([128, 4, 128]),
                     dtA_b[:, 4 * g:4 * g + 4].us(2).bc([128, 4, 128]), ALU.mult)
                pseg = psr.get()
                for hh in range(4):
                    S.mm(pseg[:, hh * 128:hh * 128 + 128], lt[:, hh * 128:hh * 128 + 128], tri, acc=(hh > 0))
                es = ft.get()
                S.act(es.v, pseg.v, AF.Exp)
                S.copy(d2e[:, 4 * g:4 * g + 4], es.v.r("p (h l) -> p h l", h=4)[:, :, 127], eng=POOL, acc=(g > 0))
                mt = bft.get()
                S.tt(mt.v.r("p (h l) -> p h l", h=4), es.v.r("p (h l) -> p h l", h=4),
                     cbm[:, gg * 128:gg * 128 + 128].us(1).bc([128, 4, 128]), ALU.mult)
                py = psr.get()
                for hh in range(4):
                    hd = 4 * g + hh
                    S.mm(py[:, hh * 64:hh * 64 + 64], mt[:, hh * 128:hh * 128 + 128], xdt_v[:, hd * 64:hd * 64 + 64], acc=(hh > 0))
                S.mm(py[:, 256:512], Cf[g][:, cs_], stbf[g].v, acc=True)
                yo = ft.get()
                S.tt(yo[:, 0:256].r("p (h d) -> p h d", h=4), py[:, 256:512].r("p (h d) -> p h d", h=4),
                     ecs[:, 4 * g:4 * g + 4].us(2).bc([128, 4, 64]), ALU.mult)
                S.tt(ycols[:, 256 * g:256 * g + 256], py[:, 0:256], yo[:, 0:256], ALU.add, acc=(g > 0))
        S.tt(xdtd_v.r("p (h d) -> p h d", d=64), xdt_v.r("p (h d) -> p h d", d=64), d2e[:, 0:32].us(2).bc([128, 32, 64]), ALU.mult)
        for g in range(8):
            pst = psr.get()
            S.mm(pst[:, 0:256], Btok_v[:, g * 128:g * 128 + 128], xdtd_v[:, 256 * g:256 * g + 256])
            S.tt(state[g].v.r("p (h d) -> p h d", h=4), state[g].v.r("p (h d) -> p h d", h=4),
                 cdec[:, 4 * g:4 * g + 4].us(2).bc([128, 4, 64]), ALU.mult)
            S.tt(state[g].v, state[g].v, pst[:, 0:256], ALU.add)
            S.copy(stbf[g].v, state[g].v, eng=ACT)
        for half in range(2):
            pt = psr.get()
            ptb = pt.v.bitcast(BF16)
            for c in range(8):
                cc_ = 8 * half + c
                S.tr(ptb[:, c * 128:c * 128 + 128], ycols[:, cc_ * 128:cc_ * 128 + 128], ident_bf.v, acc=(c > 0))
            for c in range(8):
                cc_ = 8 * half + c
                S.stt(xf[cc_][:, cs_], xf[cc_][:, cs_], P("dskip", cc_), ptb[:, c * 128:c * 128 + 128], ALU.mult, ALU.add)

    for ti in range(NT):
        t0 = ti * TT
        for c in range(8):
            S.dmav(SP, h[c].v, xT_d[c * 128:(c + 1) * 128, t0:t0 + TT], key=h[c])
        if stages >= 1:
            l0_mixer(ti)
        if stages >= 2:
            ffn(0)
        if stages >= 3:
            l1_mixer(ti)
        if stages >= 4:
            ffn(1)
        for c in range(8):
            S.op(SP, lambda e, c=c, t0=t0: e.dma_start(out=out_d[c * 128:(c + 1) * 128, t0:t0 + TT], in_=h[c].h[:]),
                 reads=[h[c]], writes=[outT], acc=True, dma_tile=h[c])
    S.op(SP, lambda e: None, reads=[outT])
    S.emit()
    nc._sched_stats = (S.sem_max, S.n_ops)
    return nc


def prepare_inputs(inputs, NT):
    Tn = NT * TT
    x = np.asarray(inputs["x"], np.float32)
    meta = np.asarray(inputs["meta_tokens"], np.float32)
    B = x.shape[0]
    pp = pack_params(inputs)
    cc = make_consts()
    wnames = ["l0_w_in", "l0_w_out", "l0_lru_w_a", "l0_lru_w_x", "l0_ffn_w_up", "l0_ffn_w_down",
              "l1_w_in", "l1_w_out", "l1_ffn_w_up", "l1_ffn_w_down"]
    wts = {n: np.ascontiguousarray(np.asarray(inputs[n], np.float32)) for n in wnames}
    maps = []
    nreal = min(x.shape[1], Tn - N_META)
    for b in range(B):
        xT = np.zeros((D, Tn), np.float32)
        xT[:, :N_META] = meta.T
        xT[:, N_META:N_META + nreal] = x[b, :nreal].T
        m = {"xT": xT, "pp": pp, "cc": cc}
        m.update(wts)
        maps.append(m)
    return maps, nreal


NT_FULL = (N_META + SEQ + TT - 1) // TT


def kernel(**inputs):
    maps, nreal = prepare_inputs(inputs, NT_FULL)
    nc = build(NT_FULL)
    res = run_bass_kernel_spmd(nc, maps, core_ids=list(range(len(maps))))
    out = np.stack([np.ascontiguousarray(r["outT"][:, N_META:N_META + nreal].T) for r in res.results], axis=0)
    return out.astype(np.float32)
```

```python
import numpy as np
import ml_dtypes
import concourse.bass as bass
import concourse.mybir as mybir
from concourse.bass_utils import run_bass_kernel_spmd

F32 = mybir.dt.float32
BF16 = mybir.dt.bfloat16
AF = mybir.ActivationFunctionType
ALU = mybir.AluOpType
AX = mybir.AxisListType

PE, ACT, DVE, POOL, SP = "pe", "act", "dve", "pool", "sp"
ENGS = (PE, ACT, DVE, POOL, SP)


class V:
    def __init__(self, ts, ap):
        self.ts = tuple(ts)
        self.ap = ap

    def __getitem__(self, k):
        return V(self.ts, self.ap[k])

    def r(self, pat, **kw):
        return V(self.ts, self.ap.rearrange(pat, **kw))

    def bc(self, shape):
        return V(self.ts, self.ap.broadcast_to(list(shape)))

    def us(self, axis):
        return V(self.ts, self.ap.unsqueeze(axis))

    def bitcast(self, dt):
        return V(self.ts, self.ap.bitcast(dt))


class T:
    def __init__(self, handle, name):
        self.h = handle
        self.name = name
        self.writers = set()
        self.readers = set()
        self.prev_readers = set()
        self.dma_sem = None
        self.dma_cnt = 0

    def __getitem__(self, k):
        return V((self,), self.h[k])

    @property
    def v(self):
        return V((self,), self.h[:])


def mv(tiles, base_ap):
    return V(tuple(tiles), base_ap)


class Sched:
    def __init__(self, nc):
        self.nc = nc
        self.ops = {e: [] for e in ENGS}
        self.tiles = []

    def track(self, handle, name):
        t = T(handle, name)
        self.tiles.append(t)
        return t

    def sb(self, name, shape, dtype):
        return self.track(self.nc.alloc_sbuf_tensor(name, list(shape), dtype), name)

    def ps(self, name, shape, dtype=F32):
        return self.track(self.nc.alloc_psum_tensor(name, list(shape), dtype), name)

    def op(self, eng, fn, reads=(), writes=(), acc=False, dma_tile=None):
        idx = len(self.ops[eng])
        me = (eng, idx)
        deps = set()
        for t in reads:
            deps |= t.writers
        for t in writes:
            if acc and t.writers:
                deps |= t.prev_readers
            else:
                deps |= t.writers | t.readers
        raw = set()
        for t in reads:
            raw |= t.writers
        fdeps = []
        for d in deps:
            if d[0] == eng:
                if eng == PE:
                    continue
                if eng in (ACT, DVE, POOL) and d not in raw:
                    continue
            fdeps.append(d)
        best = {}
        keep = []
        for d in fdeps:
            if self.ops[d[0]][d[1]][2] is not None:
                keep.append(d)
            elif d[0] not in best or best[d[0]][1] < d[1]:
                best[d[0]] = d
        fdeps = keep + list(best.values())
        self.ops[eng].append([fn, fdeps, dma_tile])
        for t in reads:
            t.readers.add(me)
        for t in writes:
            if acc and t.writers:
                t.writers.add(me)
            else:
                t.prev_readers = t.readers
                t.writers = {me}
                t.readers = set()
        return me

    def dma(self, eng, out_ap, in_ap, reads=(), writes=(), key=None, acc=False, **kw):
        assert key is not None
        return self.op(eng, lambda e: e.dma_start(out=out_ap, in_=in_ap, **kw),
                       reads=reads, writes=writes, acc=acc, dma_tile=key)

    @staticmethod
    def _ts(*vs):
        out = []
        for v in vs:
            if isinstance(v, V):
                out.extend(v.ts)
        return out

    @staticmethod
    def _a(v):
        return v.ap if isinstance(v, V) else v

    def act(self, out, in_, func, bias=0.0, scale=1.0, accum=None, acc=False):
        a = self._a
        kw = {}
        if accum is not None:
            kw["accum_out"] = a(accum)
        return self.op(ACT, lambda e: e.activation(out=a(out), in_=a(in_), func=func, bias=a(bias), scale=a(scale), **kw),
                       reads=self._ts(in_, bias, scale), writes=self._ts(out, accum), acc=acc)

    def tt(self, out, in0, in1, op, eng=DVE, acc=False):
        a = self._a
        return self.op(eng, lambda e: e.tensor_tensor(out=a(out), in0=a(in0), in1=a(in1), op=op),
                       reads=self._ts(in0, in1), writes=self._ts(out), acc=acc)

    def ts(self, out, in0, s1, op0, s2=None, op1=None, eng=DVE, acc=False):
        a = self._a
        if op1 is None:
            f = lambda e: e.tensor_scalar(out=a(out), in0=a(in0), scalar1=a(s1), scalar2=None, op0=op0)
        else:
            f = lambda e: e.tensor_scalar(out=a(out), in0=a(in0), scalar1=a(s1), scalar2=a(s2), op0=op0, op1=op1)
        return self.op(eng, f, reads=self._ts(in0, s1, s2), writes=self._ts(out), acc=acc)

    def stt(self, out, in0, scalar, in1, op0, op1, acc=False):
        a = self._a
        return self.op(DVE, lambda e: e.scalar_tensor_tensor(out=a(out), in0=a(in0), scalar=a(scalar), in1=a(in1), op0=op0, op1=op1),
                       reads=self._ts(in0, scalar, in1), writes=self._ts(out), acc=acc)

    def scan(self, out, d0, d1, init):
        a = self._a
        return self.op(DVE, lambda e: e.tensor_tensor_scan(out=a(out), data0=a(d0), data1=a(d1), initial=a(init), op0=ALU.mult, op1=ALU.add),
                       reads=self._ts(d0, d1, init), writes=self._ts(out))

    def copy(self, out, in_, eng=DVE, acc=False):
        a = self._a
        if eng == ACT:
            f = lambda e: e.copy(out=a(out), in_=a(in_))
        else:
            f = lambda e: e.tensor_copy(out=a(out), in_=a(in_))
        return self.op(eng, f, reads=self._ts(in_), writes=self._ts(out), acc=acc)

    def recip(self, out, in_):
        a = self._a
        return self.op(DVE, lambda e: e.reciprocal(out=a(out), in_=a(in_)), reads=self._ts(in_), writes=self._ts(out))

    def memset(self, out, val, eng=POOL, acc=False):
        a = self._a
        return self.op(eng, lambda e: e.memset(a(out), val), writes=self._ts(out), acc=acc)

    def mm(self, out, lhsT, rhs, start=True, stop=True, acc=None):
        a = self._a
        if acc is None:
            acc = not start
        return self.op(PE, lambda e: e.matmul(a(out), lhsT=a(lhsT), rhs=a(rhs), start=start, stop=stop),
                       reads=self._ts(lhsT, rhs), writes=self._ts(out), acc=acc)

    def tr(self, out, in_, ident, acc=False):
        a = self._a
        return self.op(PE, lambda e: e.transpose(a(out), a(in_), a(ident)),
                       reads=self._ts(in_, ident), writes=self._ts(out), acc=acc)

    def dmav(self, eng, out, in_, key, acc=False):
        a = self._a
        return self.op(eng, lambda e: e.dma_start(out=a(out), in_=a(in_)),
                       reads=self._ts(in_), writes=self._ts(out), acc=acc, dma_tile=key)

    def emit(self):
        nc = self.nc
        needed = set()
        for e in ENGS:
            for fn, deps, dt_ in self.ops[e]:
                for d in deps:
                    needed.add(d)
        eng_sem = {e: nc.alloc_semaphore("sem_" + e) for e in (PE, ACT, DVE, POOL)}
        sig = {}
        for e in (PE, ACT, DVE, POOL, SP):
            cnt = 0
            for i, (fn, deps, dt_) in enumerate(self.ops[e]):
                if dt_ is not None:
                    if dt_.dma_sem is None:
                        dt_.dma_sem = nc.alloc_semaphore("dsem_" + dt_.name)
                    dt_.dma_cnt += 16
                    sig[(e, i)] = ("d_" + dt_.name, dt_.dma_sem, dt_.dma_cnt, 16)
                elif (e, i) in needed:
                    assert e != SP
                    cnt += 1
                    sig[(e, i)] = ("e_" + e, eng_sem[e], cnt, 1)
        self.sem_max = {}
        for k_, s_, v_, i_ in sig.values():
            self.sem_max[k_] = max(self.sem_max.get(k_, 0), v_)
        self.n_ops = {e: len(self.ops[e]) for e in ENGS}
        with nc.Block() as block:
            def body(e):
                def run(eng):
                    seen = {}
                    for i, (fn, deps, dt_) in enumerate(self.ops[e]):
                        want = {}
                        for d in deps:
                            k, s, v, _ = sig[d]
                            if v > want.get(k, (0, None))[0]:
                                want[k] = (v, s)
                        for k, (v, s) in want.items():
                            if seen.get(k, 0) < v:
                                eng.wait_ge(s, v)
                                seen[k] = v
                        ins = fn(eng)
                        if (e, i) in sig and ins is not None:
                            k, s, v, inc = sig[(e, i)]
                            ins.then_inc(s, inc)
                return run
            block.tensor(body(PE))
            block.scalar(body(ACT))
            block.vector(body(DVE))
            block.gpsimd(body(POOL))
            block.sync(body(SP))


D = 1024
TT = 512
N_META = 16
SEQ = 8192
NEG = -30000.0
EPS = 1e-6
NQH = 16

PP_SPEC = [
    ("n0pre", 8), ("n0post", 8), ("f0pre", 8), ("f0post", 8),
    ("n1pre", 8), ("n1post", 8), ("f1pre", 8), ("f1post", 8),
    ("lru_cw", 32), ("lru_cb", 8), ("b_a", 8), ("b_x", 8), ("lam", 8),
    ("f0_cw", 132), ("f0_cb", 44), ("f1_cw", 132), ("f1_cb", 44),
    ("ssm_cw", 128), ("ssm_cb", 32), ("gnorm", 16), ("dskip", 16),
    ("sinks", 16), ("dt_bias", 32), ("a_log", 32),
]
CC_SPEC = [("ident", 128), ("tri", 128), ("stri", 128), ("abd", 2048), ("abp", 2048),
           ("Dm", 256), ("maskm", 256), ("nslope", 16), ("nb128", 16)]


def _offsets(spec):
    off, o = {}, 0
    for n, c in spec:
        off[n] = o
        o += c
    return off, o


PP_OFF, NPP = _offsets(PP_SPEC)
CC_OFF, NCC = _offsets(CC_SPEC)


def _chunk(v):
    v = np.asarray(v, np.float32)
    return np.ascontiguousarray(v.reshape(-1, 128).T)


def _row(v):
    v = np.asarray(v, np.float32)
    return np.ascontiguousarray(np.broadcast_to(v[None, :], (128, v.shape[0])))


def pack_params(inp):
    pp = np.zeros((128, NPP), np.float32)

    def put(name, arr):
        pp[:, PP_OFF[name]:PP_OFF[name] + arr.shape[1]] = arr
    put("n0pre", _chunk(inp["l0_mix_pre_norm"])); put("n0post", _chunk(inp["l0_mix_post_norm"]))
    put("f0pre", _chunk(inp["l0_ffn_pre_norm"])); put("f0post", _chunk(inp["l0_ffn_post_norm"]))
    put("n1pre", _chunk(inp["l1_mix_pre_norm"])); put("n1post", _chunk(inp["l1_mix_post_norm"]))
    put("f1pre", _chunk(inp["l1_ffn_pre_norm"])); put("f1post", _chunk(inp["l1_ffn_post_norm"]))
    put("lru_cw", np.concatenate([_chunk(inp["l0_lru_conv_w"][k]) for k in range(4)], axis=1))
    put("lru_cb", _chunk(inp["l0_lru_conv_b"]))
    put("b_a", _chunk(inp["l0_lru_b_a"])); put("b_x", _chunk(inp["l0_lru_b_x"])); put("lam", _chunk(inp["l0_lru_lambda"]))
    for l in (0, 1):
        put(f"f{l}_cw", np.concatenate([_chunk(inp[f"l{l}_ffn_conv_w"][k]) for k in range(3)], axis=1))
        put(f"f{l}_cb", _chunk(inp[f"l{l}_ffn_conv_b"]))
    put("ssm_cw", np.concatenate([_chunk(inp["l1_ssm_conv_w"][k]) for k in range(4)], axis=1))
    put("ssm_cb", _chunk(inp["l1_ssm_conv_b"]))
    put("gnorm", _chunk(inp["l1_gate_norm"]))
    put("dskip", _chunk(np.repeat(np.asarray(inp["l1_d_skip"], np.float32), 64)))
    put("sinks", _row(inp["l0_attn_sinks"])); put("dt_bias", _row(inp["l1_dt_bias"])); put("a_log", _row(inp["l1_a_log"]))
    return pp


def make_consts():
    cc = np.zeros((128, NCC), np.float32)

    def put(name, arr):
        cc[:, CC_OFF[name]:CC_OFF[name] + arr.shape[1]] = arr
    idx = np.arange(128)
    put("ident", np.eye(128, dtype=np.float32))
    put("tri", (idx[:, None] <= idx[None, :]).astype(np.float32))
    put("stri", (idx[None, :] < idx[:, None]).astype(np.float32))
    slopes = (2.0 ** (-8.0 * np.arange(1, NQH + 1, dtype=np.float32) / NQH)).astype(np.float32)
    tk = idx[:, None].astype(np.float32)
    tq = idx[None, :].astype(np.float32)
    dd = tq - tk
    dp = tq + 128.0 - tk
    abd = np.zeros((128, NQH, 128), np.float32)
    abp = np.zeros((128, NQH, 128), np.float32)
    for h in range(NQH):
        abd[:, h, :] = np.where(dd >= 0, -slopes[h] * dd, NEG)
        abp[:, h, :] = np.where(dp < 128, -slopes[h] * dp, NEG)
    put("abd", abd.reshape(128, -1)); put("abp", abp.reshape(128, -1))
    t = np.arange(256)[None, :].astype(np.float32)
    s_ = idx[:, None].astype(np.float32)
    dm = np.minimum(t - s_, 128.0)
    ok = (t - s_) >= 0
    put("Dm", np.where(ok, dm, 0.0).astype(np.float32))
    put("maskm", np.where(ok, 0.0, NEG).astype(np.float32))
    put("nslope", _row(-slopes)); put("nb128", _row(-128.0 * slopes))
    return cc


class Rot:
    def __init__(self, items):
        self.items = list(items)
        self.i = 0

    def get(self):
        t = self.items[self.i % len(self.items)]
        self.i += 1
        return t


def build(NT, stages=4, nslot=5):
    nc = bass.Bass("TRN2", target_bir_lowering=False)
    S = Sched(nc)
    Tn = NT * TT
    dram = lambda n, sh, kind="ExternalInput": nc.dram_tensor(n, list(sh), F32, kind=kind).ap()
    xT_d = dram("xT", [D, Tn])
    out_d = dram("outT", [D, Tn], "ExternalOutput")
    pp_d = dram("pp", [128, NPP])
    cc_d = dram("cc", [128, NCC])
    W = {n: dram(n, sh) for n, sh in [
        ("l0_w_in", [1024, 3328]), ("l0_w_out", [2048, 1024]), ("l0_lru_w_a", [8, 128, 128]), ("l0_lru_w_x", [8, 128, 128]),
        ("l0_ffn_w_up", [1024, 5632]), ("l0_ffn_w_down", [2816, 1024]),
        ("l1_w_in", [1024, 6176]), ("l1_w_out", [2048, 1024]), ("l1_ffn_w_up", [1024, 5632]), ("l1_ffn_w_down", [2816, 1024])]}
    outT = S.track(out_d, "outT_dram")

    pp = S.sb("pp_s", [128, NPP], F32)
    cc = S.sb("cc_s", [128, NCC], F32)
    P = lambda n, c=0, w=1: pp[:, PP_OFF[n] + c:PP_OFF[n] + c + w]
    C = lambda n, a=0, b=None: cc[:, CC_OFF[n] + a:CC_OFF[n] + (dict(CC_SPEC)[n] if b is None else b)]
    drv = S.sb("drv", [128, 8 + 32 + 16], F32)
    m8sp = lambda c: drv[:, c:c + 1]
    Aneg = drv[:, 8:40]
    esink = lambda h: drv[:, 40 + h:41 + h]
    ones_bf = S.sb("ones_bf", [128, 128], BF16)
    ones_f = S.sb("ones_f", [128, 128], F32)
    ident_bf = S.sb("ident_bf", [128, 128], BF16)
    wa_bf = S.sb("wa_bf", [128, 8, 128], BF16)
    wx_bf = S.sb("wx_bf", [128, 8, 128], BF16)
    wdt_bf = S.sb("wdt_bf", [128, 8, 32], BF16)
    h = [S.sb(f"h{c}", [128, TT], F32) for c in range(8)]
    xn = [S.sb(f"xn{c}", [128, TT], BF16) for c in range(8)]
    NBFA, NBFT, NFT = 46, 6, 14
    bfa_h = nc.alloc_sbuf_tensor("bfa", [128, NBFA * TT], BF16)
    BFA = [S.track(bfa_h[:, i * TT:(i + 1) * TT], f"bfa{i}") for i in range(NBFA)]
    bfm = lambda i, n: mv(BFA[i:i + n], bfa_h[:, i * TT:(i + n) * TT])
    bft = Rot([S.sb(f"bft{i}", [128, TT], BF16) for i in range(NBFT)])
    ft = Rot([S.sb(f"ft{i}", [128, TT], F32) for i in range(NFT)])
    xcb = Rot([S.sb(f"xcb{i}", [128, TT + 4], F32) for i in range(3)])
    ring = [S.sb(f"ring{i}", [128, 8, TT], BF16) for i in range(nslot)]
    PSB = [S.ps(f"psb{i}", [128, TT]) for i in range(8)]
    psr = Rot(PSB[:7])
    pstat = PSB[7]
    sml = Rot([S.sb(f"sml{i}", [128, 128], F32) for i in range(8)])
    lruh_h = nc.alloc_sbuf_tensor("lruh", [128, 8 * 3], F32)
    lruh = [S.track(lruh_h[:, 3 * c:3 * c + 3], f"lruh{c}") for c in range(8)]
    lrus_h = nc.alloc_sbuf_tensor("lrus", [128, 8], F32)
    lrus = [S.track(lrus_h[:, c:c + 1], f"lrus{c}") for c in range(8)]
    kbuf = [S.sb(f"kbuf{g}", [128, 128 + TT], BF16) for g in range(2)]
    vbuf = S.sb("vbuf", [128, 5, 128], BF16)
    kmeta = [S.sb(f"kmeta{g}", [128, 16], BF16) for g in range(2)]
    vmeta = S.sb("vmeta", [128, 128], BF16)
    fh_h = [nc.alloc_sbuf_tensor(f"fh{l}", [128, 44 * 2], F32) for l in range(2)]
    fh = [[S.track(fh_h[l][:, 2 * c:2 * c + 2], f"fh{l}_{c}") for c in range(44)] for l in range(2)]
    sh_h = nc.alloc_sbuf_tensor("sh", [128, 32 * 3], F32)
    sh = [S.track(sh_h[:, 3 * c:3 * c + 3], f"sh{c}") for c in range(32)]
    st_h = nc.alloc_sbuf_tensor("state", [128, 2048], F32)
    state = [S.track(st_h[:, 256 * g:256 * g + 256], f"state{g}") for g in range(8)]
    stb_h = nc.alloc_sbuf_tensor("stbf", [128, 2048], BF16)
    stbf = [S.track(stb_h[:, 256 * g:256 * g + 256], f"stbf{g}") for g in range(8)]
    dts = S.sb("dts", [128, 4, 32], F32)
    dtA = S.sb("dtA", [128, 4, 32], F32)

    S.dmav(SP, pp.v, pp_d[:, :], key=pp)
    S.dmav(SP, cc.v, cc_d[:, :], key=cc)
    S.dmav(POOL, wa_bf.v, W["l0_lru_w_a"].rearrange("n i j -> i n j"), key=wa_bf)
    S.dmav(POOL, wx_bf.v, W["l0_lru_w_x"].rearrange("n i j -> i n j"), key=wx_bf)
    S.dmav(POOL, wdt_bf.v, W["l1_w_in"][:, 6144:6176].rearrange("(k p) n -> p k n", p=128), key=wdt_bf)
    S.memset(ones_bf.v, 1.0)
    S.memset(ones_f.v, 1.0)
    S.copy(ident_bf.v, C("ident"))
    for t_ in lruh + lrus + sh + state + stbf + [x for l in fh for x in l] + kbuf + kmeta + [vbuf, vmeta]:
        S.memset(t_.v, 0.0)
    t0_ = sml.get()
    S.act(t0_[:, 0:8], P("lam", 0, 8), AF.Exp, scale=-1.0)
    S.act(t0_[:, 8:16], t0_[:, 0:8], AF.Ln, bias=1.0)
    S.ts(drv[:, 0:8], t0_[:, 8:16], -8.0, ALU.mult)
    S.act(t0_[:, 16:48], P("a_log", 0, 32), AF.Exp)
    S.ts(drv[:, 8:40], t0_[:, 16:48], -1.0, ALU.mult)
    S.act(drv[:, 40:56], P("sinks", 0, 16), AF.Exp)

    ucount = [0]

    ucache = {}

    def load_unit(wname, r0, nk, c0, ncol):
        slot = ring[ucount[0] % nslot]
        ucount[0] += 1
        key = (wname, r0, nk, c0, ncol)
        if key not in ucache:
            nm = f"U{len(ucache)}"
            uh = nc.dram_tensor(nm, [128, nk * ncol], BF16, kind="Internal").ap()
            ut = S.track(uh, nm)
            first = True
            for k0 in range(0, nk, 4):
                k1 = min(nk, k0 + 4)
                S.op(POOL, lambda e, uh=uh, k0=k0, k1=k1: e.dma_start(
                    out=uh[:, k0 * ncol:k1 * ncol].rearrange("p (k n) -> p k n", n=ncol),
                    in_=W[wname][r0 + k0 * 128:r0 + k1 * 128, c0:c0 + ncol].rearrange("(k p) n -> p k n", p=128)),
                    writes=[ut], acc=not first, dma_tile=ut)
                first = False
            ucache[key] = (uh, ut)
        uh, ut = ucache[key]
        S.op(SP, lambda e, uh=uh: e.dma_start(out=slot.h[:, 0:nk, 0:ncol], in_=uh[:, :].rearrange("p (k n) -> p k n", n=ncol)),
             reads=[ut], writes=[slot], dma_tile=slot)
        return slot

    def rmsnorm_to_xn(wname):
        for c in range(8):
            sq = bft.get()
            S.act(sq.v, h[c].v, AF.Square)
            S.mm(pstat.v, ones_bf.v, sq.v, start=(c == 0), stop=(c == 7))
        rs = ft.get()
        S.act(rs.v, pstat.v, AF.Sqrt, bias=EPS_AP(), scale=1.0 / D)
        S.recip(rs.v, rs.v)
        for c in range(8):
            S.stt(xn[c].v, h[c].v, P(wname, c), rs.v, ALU.mult, ALU.mult)

    eps_t = S.sb("eps_t", [128, 1], F32)
    S.memset(eps_t.v, EPS)
    EPS_AP = lambda: eps_t[:, 0:1]

    class Epi:
        def __init__(self, wname):
            self.w = wname
            self.mo = []

        def chunk(self, oc, ps):
            mo = ft.get()
            S.copy(mo.v, ps.v, eng=ACT)
            sq = bft.get()
            S.act(sq.v, ps.v, AF.Square)
            S.mm(pstat.v, ones_bf.v, sq.v, start=(oc == 0), stop=(oc == 7))
            self.mo.append(mo)

        def finish(self):
            rs = ft.get()
            S.act(rs.v, pstat.v, AF.Sqrt, bias=EPS_AP(), scale=1.0 / D)
            S.recip(rs.v, rs.v)
            for c in range(8):
                S.stt(self.mo[c].v, self.mo[c].v, P(self.w, c), rs.v, ALU.mult, ALU.mult)
                S.tt(h[c].v, h[c].v, self.mo[c].v, ALU.add)

    def conv_chunk(ps, halo, K, wname, bname, cidx, nch):
        xb = xcb.get()
        S.copy(xb[:, 0:K - 1], halo.v, eng=POOL)
        S.copy(xb[:, K - 1:K - 1 + TT], ps.v, eng=ACT)
        S.copy(halo.v, xb[:, TT:TT + K - 1], eng=POOL)
        cv = ft.get()
        wcol = lambda k: P(wname, k * nch + cidx)
        S.act(cv.v, ps.v, AF.Identity, bias=P(bname, cidx), scale=wcol(K - 1))
        for k in range(K - 2, -1, -1):
            S.stt(cv.v, xb[:, k:k + TT], wcol(k), cv.v, ALU.mult, ALU.add)
        return cv

    def proj_chunk(slot_of_k, j, rhs_list, ncolj=128):
        ps = psr.get()
        n = len(rhs_list)
        for k in range(n):
            sl, kk = slot_of_k(k)
            S.mm(ps[0:ncolj, :], sl[:, kk, j * 128:j * 128 + ncolj], rhs_list[k].v, start=(k == 0), stop=(k == n - 1))
        return ps

    def l0_mixer(ti):
        rmsnorm_to_xn("n0pre")
        gate, q, ya, yb = BFA[0:8], BFA[8:16], BFA[16:24], BFA[24:32]
        for u in range(7):
            ncol = 512 if u < 6 else 256
            slot = load_unit("l0_w_in", 0, 8, 512 * u, ncol)
            sk = lambda k, slot=slot: (slot, k)
            if u < 6:
                for j in range(4):
                    c = (u % 2) * 4 + j
                    ps = proj_chunk(sk, j, xn)
                    if u < 2:
                        S.act(gate[c].v, ps.v, AF.Gelu_apprx_tanh)
                    elif u < 4:
                        lru_chunk(c, ps, gate[c], ya[c])
                    else:
                        S.copy(q[c].v, ps.v, eng=ACT)
            else:
                for g in range(2):
                    ps = psr.get()
                    for half in range(2):
                        for k in range(8):
                            S.mm(ps[64 * half:64 * half + 64, :], slot[:, k, 64 * g:64 * g + 64], xn[k].v,
                                 start=(k == 0), stop=(k == 7), acc=not (half == 0 and k == 0))
                    S.copy(kbuf[g][:, 128:128 + TT], ps.v, eng=ACT)
                    if ti == 0:
                        S.copy(kmeta[g].v, kbuf[g][:, 128:144])
                ps = psr.get()
                for blk in range(4):
                    for k in range(8):
                        S.mm(ps[:, blk * 128:blk * 128 + 128], xn[k][:, blk * 128:blk * 128 + 128], slot[:, k, 128:256],
                             start=(k == 0), stop=(k == 7), acc=not (blk == 0 and k == 0))
                S.copy(vbuf[:, 1:5, :], ps.v.r("p (b n) -> p b n", b=4), eng=ACT)
                if ti == 0:
                    S.copy(vmeta[0:16, :], vbuf[0:16, 1, :])
        attention(ti, q, yb)
        for g in range(2):
            S.copy(kbuf[g][:, 0:128], kbuf[g][:, TT:TT + 128], eng=POOL)
        S.copy(vbuf[:, 0, :], vbuf[:, 4, :], eng=POOL)
        epi = Epi("n0post")
        for cg in range(2):
            ua = load_unit("l0_w_out", 0, 8, 512 * cg, 512)
            ub = load_unit("l0_w_out", 1024, 8, 512 * cg, 512)
            sk = lambda k, ua=ua, ub=ub: (ua, k) if k < 8 else (ub, k - 8)
            for j in range(4):
                ps = proj_chunk(sk, j, list(ya) + list(yb))
                epi.chunk(4 * cg + j, ps)
        epi.finish()

    def lru_chunk(c, ps, gate_c, ya_c):
        cv = conv_chunk(ps, lruh[c], 4, "lru_cw", "lru_cb", c, 8)
        xvb = bft.get()
        S.copy(xvb.v, cv.v, eng=ACT)
        pr, pi = psr.get(), psr.get()
        S.mm(pr.v, wa_bf[:, c, :], xvb.v)
        S.mm(pi.v, wx_bf[:, c, :], xvb.v)
        r, i_, a, om = ft.get(), ft.get(), ft.get(), ft.get()
        S.act(r.v, pr.v, AF.Sigmoid, bias=P("b_a", c))
        S.act(i_.v, pi.v, AF.Sigmoid, bias=P("b_x", c))
        S.act(a.v, r.v, AF.Exp, scale=m8sp(c))
        S.tt(om.v, a.v, a.v, ALU.mult)
        S.ts(om.v, om.v, -1.0, ALU.mult, 1.0, ALU.add)
        S.act(om.v, om.v, AF.Sqrt)
        S.tt(i_.v, i_.v, cv.v, ALU.mult)
        S.tt(i_.v, i_.v, om.v, ALU.mult)
        hs = ft.get()
        S.scan(hs.v, a.v, i_.v, lrus[c].v)
        S.copy(lrus[c].v, hs[:, TT - 1:TT], eng=POOL)
        S.tt(ya_c.v, gate_c.v, hs.v, ALU.mult)

    def attention(ti, q, yb):
        for hp in range(8):
            g = hp // 4
            for blk in range(4):
                n = 4 * ti + blk
                ps, pm, pb = psr.get(), psr.get(), psr.get()
                for e in range(2):
                    b0 = 64 * e
                    qh = q[hp][b0:b0 + 64, blk * 128:blk * 128 + 128]
                    first = (e == 0)
                    S.mm(ps[:, e * 128:e * 128 + 128], kbuf[g][b0:b0 + 64, 128 + blk * 128:256 + blk * 128], qh, acc=not first)
                    S.mm(ps[:, 256 + e * 128:384 + e * 128], kbuf[g][b0:b0 + 64, blk * 128:blk * 128 + 128], qh, acc=True)
                    S.mm(pm[0:16, e * 128:e * 128 + 128], kmeta[g][b0:b0 + 64, 0:16], qh, acc=not first)
                sb = ft.get()
                S.stt(sb[:, 0:256], ps[:, 0:256], 0.125, C("abd", 256 * hp, 256 * hp + 256), ALU.mult, ALU.add)
                S.stt(sb[:, 256:512], ps[:, 256:512], 0.125, C("abp", 256 * hp, 256 * hp + 256), ALU.mult, ALU.add, acc=True)
                pT = bft.get()
                S.act(pT.v, sb.v, AF.Exp)
                if n == 0:
                    S.memset(pT[:, 256:512], 0.0, eng=DVE)
                    S.memset(pT[0:16, 0:256], 0.0, eng=DVE)
                elif n == 1:
                    S.memset(pT[0:16, 256:512], 0.0, eng=DVE)
                pTm = bft.get()
                for e in range(2):
                    hh = 2 * hp + e
                    dstm = pTm[0:16, e * 128:e * 128 + 128]
                    srcm = pm[0:16, e * 128:e * 128 + 128]
                    if n >= 2:
                        S.act(dstm, srcm, AF.Exp, bias=C("nb128", hh, hh + 1)[0:16, :], scale=0.125, acc=(e == 1))
                    else:
                        tm = sml.get()
                        S.stt(tm[0:16, :], C("Dm", n * 128, n * 128 + 128)[0:16, :], C("nslope", hh, hh + 1)[0:16, :],
                              C("maskm", n * 128, n * 128 + 128)[0:16, :], ALU.mult, ALU.add)
                        S.stt(tm[0:16, :], srcm, 0.125, tm[0:16, :], ALU.mult, ALU.add)
                        S.act(dstm, tm[0:16, :], AF.Exp, acc=(e == 1))
                for e in range(2):
                    b0 = 64 * e
                    for o0 in (0, 256):
                        dst = pb[b0:b0 + 64, o0 + e * 128:o0 + e * 128 + 128]
                        if o0 == 0:
                            l1, l2, l3 = vbuf[:, 1 + blk, 64 * g:64 * g + 64], vbuf[:, blk, 64 * g:64 * g + 64], vmeta[0:16, 64 * g:64 * g + 64]
                        else:
                            l1, l2, l3 = ones_bf[:, 0:64], ones_bf[:, 0:64], ones_bf[0:16, 0:64]
                        S.mm(dst, l1, pT[:, e * 128:e * 128 + 128], start=True, stop=False, acc=not (e == 0 and o0 == 0))
                        S.mm(dst, l2, pT[:, 256 + e * 128:384 + e * 128], start=False, stop=False)
                        S.mm(dst, l3, pTm[0:16, e * 128:e * 128 + 128], start=False, stop=True)
                dn = sml.get()
                for e in range(2):
                    b0 = 64 * e
                    S.ts(dn[b0:b0 + 64, :], pb[b0:b0 + 64, 256 + e * 128:384 + e * 128], esink(2 * hp + e)[b0:b0 + 64, :], ALU.add, acc=(e == 1))
                S.recip(dn.v, dn.v)
                for e in range(2):
                    b0 = 64 * e
                    S.tt(yb[hp][b0:b0 + 64, blk * 128:blk * 128 + 128], pb[b0:b0 + 64, e * 128:e * 128 + 128], dn[b0:b0 + 64, :], ALU.mult, acc=True)

    def ffn(l):
        rmsnorm_to_xn(f"f{l}pre")
        actb = BFA[0:22]
        for u in range(11):
            slot = load_unit(f"l{l}_ffn_w_up", 0, 8, 512 * u, 512)
            sk = lambda k, slot=slot: (slot, k)
            for j in range(4):
                ci = 4 * u + j
                ps = proj_chunk(sk, j, xn)
                cv = conv_chunk(ps, fh[l][ci], 3, f"f{l}_cw", f"f{l}_cb", ci, 44)
                if ci < 22:
                    S.act(actb[ci].v, cv.v, AF.Gelu_apprx_tanh)
                else:
                    S.tt(actb[ci - 22].v, actb[ci - 22].v, cv.v, ALU.mult)
        epi = Epi(f"f{l}post")
        for cg in range(2):
            us = [load_unit(f"l{l}_ffn_w_down", 1024 * i, 8 if i < 2 else 6, 512 * cg, 512) for i in range(3)]
            sk = lambda k, us=us: (us[k // 8], k % 8)
            for j in range(4):
                ps = proj_chunk(sk, j, actb)
                epi.chunk(4 * cg + j, ps)
        epi.finish()

    def l1_mixer(ti):
        rmsnorm_to_xn("n1pre")
        xf, Bf, Cf = BFA[0:16], BFA[16:24], BFA[24:32]
        xdt_v, xdtd_v, Btok_v, ytok_v = bfm(32, 4), bfm(36, 4), bfm(40, 2), bfm(42, 4)
        tri, stri = C("tri"), C("stri")
        for blk in range(4):
            ps = psr.get()
            for k in range(8):
                S.mm(ps[:, 0:32], xn[k][:, blk * 128:blk * 128 + 128], wdt_bf[:, k, :], start=(k == 0), stop=(k == 7))
            t1 = sml.get()
            S.tt(t1[:, 0:32], ps[:, 0:32], P("dt_bias", 0, 32), ALU.add)
            S.act(t1[:, 32:64], t1[:, 0:32], AF.Exp)
            S.act(dts[:, blk, :], t1[:, 32:64], AF.Ln, bias=1.0, acc=(blk > 0))
            S.tt(dtA[:, blk, :], dts[:, blk, :], Aneg, ALU.mult, acc=(blk > 0))
        for u in range(8):
            slot = load_unit("l1_w_in", 0, 8, 2048 + 512 * u, 512)
            sk = lambda k, slot=slot: (slot, k)
            for j in range(4):
                cc_ = 4 * u + j
                ps = proj_chunk(sk, j, xn)
                cv = conv_chunk(ps, sh[cc_], 4, "ssm_cw", "ssm_cb", cc_, 32)
                dst = xf[cc_] if cc_ < 16 else (Bf[cc_ - 16] if cc_ < 24 else Cf[cc_ - 24])
                S.act(dst.v, cv.v, AF.Silu)
        for blk in range(4):
            ssd_chunk(blk, xf, Bf, Cf, xdt_v, xdtd_v, Btok_v, ytok_v, tri, stri)
        for u in range(4):
            slot = load_unit("l1_w_in", 0, 8, 512 * u, 512)
            sk = lambda k, slot=slot: (slot, k)
            for j in range(4):
                c = 4 * u + j
                ps = proj_chunk(sk, j, xn)
                sz = ft.get()
                S.act(sz.v, ps.v, AF.Silu)
                S.tt(xf[c].v, xf[c].v, sz.v, ALU.mult)
                if c % 2 == 1:
                    pg = psr.get()
                    for i2, c2 in enumerate((c - 1, c)):
                        sq = bft.get()
                        S.act(sq.v, xf[c2].v, AF.Square)
                        S.mm(pg.v, ones_bf.v, sq.v, start=(i2 == 0), stop=(i2 == 1))
                    rs = ft.get()
                    S.act(rs.v, pg.v, AF.Sqrt, bias=EPS_AP(), scale=1.0 / 256)
                    S.recip(rs.v, rs.v)
                    for c2 in (c - 1, c):
                        S.stt(xf[c2].v, xf[c2].v, P("gnorm", c2), rs.v, ALU.mult, ALU.mult)
        epi = Epi("n1post")
        for cg in range(2):
            us = [load_unit("l1_w_out", 1024 * i, 8, 512 * cg, 512) for i in range(2)]
            sk = lambda k, us=us: (us[k // 8], k % 8)
            for j in range(4):
                ps = proj_chunk(sk, j, xf)
                epi.chunk(4 * cg + j, ps)
        epi.finish()

    def ssd_chunk(blk, xf, Bf, Cf, xdt_v, xdtd_v, Btok_v, ytok_v, tri, stri):
        cs_ = slice(blk * 128, blk * 128 + 128)
        dt_b = dts[:, blk, :]
        dtA_b = dtA[:, blk, :]
        for half in range(2):
            pt = psr.get()
            ptb = pt.v.bitcast(BF16)
            for c in range(8):
                S.tr(ptb[:, c * 128:c * 128 + 128], xf[8 * half + c][:, cs_], ident_bf.v, acc=(c > 0))
            S.tt(xdt_v[:, half * 1024:half * 1024 + 1024].r("p (h d) -> p h d", d=64), ptb.r("p (h d) -> p h d", d=64),
                 dt_b[:, 16 * half:16 * half + 16].us(2).bc([128, 16, 64]), ALU.mult, acc=(half == 1))
        pt = psr.get()
        ptb = pt.v.bitcast(BF16)
        for g in range(8):
            S.tr(ptb[:, g * 128:g * 128 + 128], Bf[g][:, cs_], ident_bf.v, acc=(g > 0))
        S.copy(Btok_v, ptb, eng=ACT)
        pc = psr.get()
        S.mm(pc[:, 0:32], tri, dtA_b)
        S.mm(pc[:, 32:64], ones_f.v, dtA_b, acc=True)
        sm = sml.get()
        S.act(sm[:, 0:64], pc[:, 0:64], AF.Exp)
        ecs, cdec = sm[:, 0:32], sm[:, 32:64]
        d2e = sml.get()
        ycols = ytok_v
        for half in range(2):
            pcb = psr.get()
            for gg in range(4):
                g = 4 * half + gg
                S.mm(pcb[:, gg * 128:gg * 128 + 128], Bf[g][:, cs_], Cf[g][:, cs_], acc=(gg > 0))
            cbm = ft.get()
            S.tt(cbm.v.r("p (g l) -> p g l", g=4), pcb.v.r("p (g l) -> p g l", g=4), tri.us(1).bc([128, 4, 128]), ALU.mult)
            for gg in range(4):
                g = 4 * half + gg
                lt = ft.get()
                S.tt(lt.v.r("p (h s) -> p h s", h=4), stri.us(1).bc([128, 4, 128]),
                     dtA_b[:, 4 * g:4 * g + 4].us(2).bc([128, 4, 128]), ALU.mult)
                pseg = psr.get()
                for hh in range(4):
                    S.mm(pseg[:, hh * 128:hh * 128 + 128], lt[:, hh * 128:hh * 128 + 128], tri, acc=(hh > 0))
                es = ft.get()
                S.act(es.v, pseg.v, AF.Exp)
                S.copy(d2e[:, 4 * g:4 * g + 4], es.v.r("p (h l) -> p h l", h=4)[:, :, 127], eng=POOL, acc=(g > 0))
                mt = bft.get()
                S.tt(mt.v.r("p (h l) -> p h l", h=4), es.v.r("p (h l) -> p h l", h=4),
                     cbm[:, gg * 128:gg * 128 + 128].us(1).bc([128, 4, 128]), ALU.mult)
                py = psr.get()
                for hh in range(4):
                    hd = 4 * g + hh
                    S.mm(py[:, hh * 64:hh * 64 + 64], mt[:, hh * 128:hh * 128 + 128], xdt_v[:, hd * 64:hd * 64 + 64], acc=(hh > 0))
                S.mm(py[:, 256:512], Cf[g][:, cs_], stbf[g].v, acc=True)
                yo = ft.get()
                S.tt(yo[:, 0:256].r("p (h d) -> p h d", h=4), py[:, 256:512].r("p (h d) -> p h d", h=4),
                     ecs[:, 4 * g:4 * g + 4].us(2).bc([128, 4, 64]), ALU.mult)
                S.tt(ycols[:, 256 * g:256 * g + 256], py[:, 0:256], yo[:, 0:256], ALU.add, acc=(g > 0))
        S.tt(xdtd_v.r("p (h d) -> p h d", d=64), xdt_v.r("p (h d) -> p h d", d=64), d2e[:, 0:32].us(2).bc([128, 32, 64]), ALU.mult)
        for g in range(8):
            pst = psr.get()
            S.mm(pst[:, 0:256], Btok_v[:, g * 128:g * 128 + 128], xdtd_v[:, 256 * g:256 * g + 256])
            S.tt(state[g].v.r("p (h d) -> p h d", h=4), state[g].v.r("p (h d) -> p h d", h=4),
                 cdec[:, 4 * g:4 * g + 4].us(2).bc([128, 4, 64]), ALU.mult)
            S.tt(state[g].v, state[g].v, pst[:, 0:256], ALU.add)
            S.copy(stbf[g].v, state[g].v, eng=ACT)
        for half in range(2):
            pt = psr.get()
            ptb = pt.v.bitcast(BF16)
            for c in range(8):
                cc_ = 8 * half + c
                S.tr(ptb[:, c * 128:c * 128 + 128], ycols[:, cc_ * 128:cc_ * 128 + 128], ident_bf.v, acc=(c > 0))
            for c in range(8):
                cc_ = 8 * half + c
                S.stt(xf[cc_][:, cs_], xf[cc_][:, cs_], P("dskip", cc_), ptb[:, c * 128:c * 128 + 128], ALU.mult, ALU.add)

    for ti in range(NT):
        t0 = ti * TT
        for c in range(8):
            S.dmav(SP, h[c].v, xT_d[c * 128:(c + 1) * 128, t0:t0 + TT], key=h[c])
        if stages >= 1:
            l0_mixer(ti)
        if stages >= 2:
            ffn(0)
        if stages >= 3:
            l1_mixer(ti)
        if stages >= 4:
            ffn(1)
        for c in range(8):
            S.op(SP, lambda e, c=c, t0=t0: e.dma_start(out=out_d[c * 128:(c + 1) * 128, t0:t0 + TT], in_=h[c].h[:]),
                 reads=[h[c]], writes=[outT], acc=True, dma_tile=h[c])
    S.op(SP, lambda e: None, reads=[outT])
    S.emit()
    nc._sched_stats = (S.sem_max, S.n_ops)
    return nc


def prepare_inputs(inputs, NT):
    Tn = NT * TT
    x = np.asarray(inputs["x"], np.float32)
    meta = np.asarray(inputs["meta_tokens"], np.float32)
    B = x.shape[0]
    pp = pack_params(inputs)
    cc = make_consts()
    wnames = ["l0_w_in", "l0_w_out", "l0_lru_w_a", "l0_lru_w_x", "l0_ffn_w_up", "l0_ffn_w_down",
              "l1_w_in", "l1_w_out", "l1_ffn_w_up", "l1_ffn_w_down"]
    wts = {n: np.ascontiguousarray(np.asarray(inputs[n], np.float32)) for n in wnames}
    maps = []
    nreal = min(x.shape[1], Tn - N_META)
    for b in range(B):
        xT = np.zeros((D, Tn), np.float32)
        xT[:, :N_META] = meta.T
        xT[:, N_META:N_META + nreal] = x[b, :nreal].T
        m = {"xT": xT, "pp": pp, "cc": cc}
        m.update(wts)
        maps.append(m)
    return maps, nreal


NT_FULL = (N_META + SEQ + TT - 1) // TT


def kernel(**inputs):
    maps, nreal = prepare_inputs(inputs, NT_FULL)
    nc = build(NT_FULL)
    res = run_bass_kernel_spmd(nc, maps, core_ids=list(range(len(maps))))
    out = np.stack([np.ascontiguousarray(r["outT"][:, N_META:N_META + nreal].T) for r in res.results], axis=0)
    return out.astype(np.float32)
```

```python
import numpy as np
import ml_dtypes
import concourse.bass as bass
import concourse.mybir as mybir
from concourse.bass_utils import run_bass_kernel_spmd

F32 = mybir.dt.float32
BF16 = mybir.dt.bfloat16
AF = mybir.ActivationFunctionType
ALU = mybir.AluOpType
AX = mybir.AxisListType

PE, ACT, DVE, POOL, SP = "pe", "act", "dve", "pool", "sp"
ENGS = (PE, ACT, DVE, POOL, SP)


class V:
    def __init__(self, ts, ap):
        self.ts = tuple(ts)
        self.ap = ap

    def __getitem__(self, k):
        return V(self.ts, self.ap[k])

    def r(self, pat, **kw):
        return V(self.ts, self.ap.rearrange(pat, **kw))

    def bc(self, shape):
        return V(self.ts, self.ap.broadcast_to(list(shape)))

    def us(self, axis):
        return V(self.ts, self.ap.unsqueeze(axis))

    def bitcast(self, dt):
        return V(self.ts, self.ap.bitcast(dt))


class T:
    def __init__(self, handle, name):
        self.h = handle
        self.name = name
        self.writers = set()
        self.readers = set()
        self.prev_readers = set()
        self.dma_sem = None
        self.dma_cnt = 0

    def __getitem__(self, k):
        return V((self,), self.h[k])

    @property
    def v(self):
        return V((self,), self.h[:])


def mv(tiles, base_ap):
    return V(tuple(tiles), base_ap)


class Sched:
    def __init__(self, nc):
        self.nc = nc
        self.ops = {e: [] for e in ENGS}
        self.tiles = []

    def track(self, handle, name):
        t = T(handle, name)
        self.tiles.append(t)
        return t

    def sb(self, name, shape, dtype):
        return self.track(self.nc.alloc_sbuf_tensor(name, list(shape), dtype), name)

    def ps(self, name, shape, dtype=F32):
        return self.track(self.nc.alloc_psum_tensor(name, list(shape), dtype), name)

    def op(self, eng, fn, reads=(), writes=(), acc=False, dma_tile=None):
        idx = len(self.ops[eng])
        me = (eng, idx)
        deps = set()
        for t in reads:
            deps |= t.writers
        for t in writes:
            if acc and t.writers:
                deps |= t.prev_readers
            else:
                deps |= t.writers | t.readers
        raw = set()
        for t in reads:
            raw |= t.writers
        fdeps = []
        for d in deps:
            if d[0] == eng:
                if eng == PE:
                    continue
                if eng in (ACT, DVE, POOL) and d not in raw:
                    continue
            fdeps.append(d)
        best = {}
        keep = []
        for d in fdeps:
            if self.ops[d[0]][d[1]][2] is not None:
                keep.append(d)
            elif d[0] not in best or best[d[0]][1] < d[1]:
                best[d[0]] = d
        fdeps = keep + list(best.values())
        self.ops[eng].append([fn, fdeps, dma_tile])
        for t in reads:
            t.readers.add(me)
        for t in writes:
            if acc and t.writers:
                t.writers.add(me)
            else:
                t.prev_readers = t.readers
                t.writers = {me}
                t.readers = set()
        return me

    def dma(self, eng, out_ap, in_ap, reads=(), writes=(), key=None, acc=False, **kw):
        assert key is not None
        return self.op(eng, lambda e: e.dma_start(out=out_ap, in_=in_ap, **kw),
                       reads=reads, writes=writes, acc=acc, dma_tile=key)

    @staticmethod
    def _ts(*vs):
        out = []
        for v in vs:
            if isinstance(v, V):
                out.extend(v.ts)
        return out

    @staticmethod
    def _a(v):
        return v.ap if isinstance(v, V) else v

    def act(self, out, in_, func, bias=0.0, scale=1.0, accum=None, acc=False):
        a = self._a
        kw = {}
        if accum is not None:
            kw["accum_out"] = a(accum)
        return self.op(ACT, lambda e: e.activation(out=a(out), in_=a(in_), func=func, bias=a(bias), scale=a(scale), **kw),
                       reads=self._ts(in_, bias, scale), writes=self._ts(out, accum), acc=acc)

    def tt(self, out, in0, in1, op, eng=DVE, acc=False):
        a = self._a
        return self.op(eng, lambda e: e.tensor_tensor(out=a(out), in0=a(in0), in1=a(in1), op=op),
                       reads=self._ts(in0, in1), writes=self._ts(out), acc=acc)

    def ts(self, out, in0, s1, op0, s2=None, op1=None, eng=DVE, acc=False):
        a = self._a
        if op1 is None:
            f = lambda e: e.tensor_scalar(out=a(out), in0=a(in0), scalar1=a(s1), scalar2=None, op0=op0)
        else:
            f = lambda e: e.tensor_scalar(out=a(out), in0=a(in0), scalar1=a(s1), scalar2=a(s2), op0=op0, op1=op1)
        return self.op(eng, f, reads=self._ts(in0, s1, s2), writes=self._ts(out), acc=acc)

    def stt(self, out, in0, scalar, in1, op0, op1, acc=False):
        a = self._a
        return self.op(DVE, lambda e: e.scalar_tensor_tensor(out=a(out), in0=a(in0), scalar=a(scalar), in1=a(in1), op0=op0, op1=op1),
                       reads=self._ts(in0, scalar, in1), writes=self._ts(out), acc=acc)

    def scan(self, out, d0, d1, init):
        a = self._a
        return self.op(DVE, lambda e: e.tensor_tensor_scan(out=a(out), data0=a(d0), data1=a(d1), initial=a(init), op0=ALU.mult, op1=ALU.add),
                       reads=self._ts(d0, d1, init), writes=self._ts(out))

    def copy(self, out, in_, eng=DVE, acc=False):
        a = self._a
        if eng == ACT:
            f = lambda e: e.copy(out=a(out), in_=a(in_))
        else:
            f = lambda e: e.tensor_copy(out=a(out), in_=a(in_))
        return self.op(eng, f, reads=self._ts(in_), writes=self._ts(out), acc=acc)

    def recip(self, out, in_):
        a = self._a
        return self.op(DVE, lambda e: e.reciprocal(out=a(out), in_=a(in_)), reads=self._ts(in_), writes=self._ts(out))

    def memset(self, out, val, eng=POOL, acc=False):
        a = self._a
        return self.op(eng, lambda e: e.memset(a(out), val), writes=self._ts(out), acc=acc)

    def mm(self, out, lhsT, rhs, start=True, stop=True, acc=None):
        a = self._a
        if acc is None:
            acc = not start
        return self.op(PE, lambda e: e.matmul(a(out), lhsT=a(lhsT), rhs=a(rhs), start=start, stop=stop),
                       reads=self._ts(lhsT, rhs), writes=self._ts(out), acc=acc)

    def tr(self, out, in_, ident, acc=False):
        a = self._a
        return self.op(PE, lambda e: e.transpose(a(out), a(in_), a(ident)),
                       reads=self._ts(in_, ident), writes=self._ts(out), acc=acc)

    def dmav(self, eng, out, in_, key, acc=False):
        a = self._a
        return self.op(eng, lambda e: e.dma_start(out=a(out), in_=a(in_)),
                       reads=self._ts(in_), writes=self._ts(out), acc=acc, dma_tile=key)

    def emit(self):
        nc = self.nc
        needed = set()
        for e in ENGS:
            for fn, deps, dt_ in self.ops[e]:
                for d in deps:
                    needed.add(d)
        eng_sem = {e: nc.alloc_semaphore("sem_" + e) for e in (PE, ACT, DVE, POOL)}
        sig = {}
        for e in (PE, ACT, DVE, POOL, SP):
            cnt = 0
            for i, (fn, deps, dt_) in enumerate(self.ops[e]):
                if dt_ is not None:
                    if dt_.dma_sem is None:
                        dt_.dma_sem = nc.alloc_semaphore("dsem_" + dt_.name)
                    dt_.dma_cnt += 16
                    sig[(e, i)] = ("d_" + dt_.name, dt_.dma_sem, dt_.dma_cnt, 16)
                elif (e, i) in needed:
                    assert e != SP
                    cnt += 1
                    sig[(e, i)] = ("e_" + e, eng_sem[e], cnt, 1)
        self.sem_max = {}
        for k_, s_, v_, i_ in sig.values():
            self.sem_max[k_] = max(self.sem_max.get(k_, 0), v_)
        self.n_ops = {e: len(self.ops[e]) for e in ENGS}
        with nc.Block() as block:
            def body(e):
                def run(eng):
                    seen = {}
                    for i, (fn, deps, dt_) in enumerate(self.ops[e]):
                        want = {}
                        for d in deps:
                            k, s, v, _ = sig[d]
                            if v > want.get(k, (0, None))[0]:
                                want[k] = (v, s)
                        for k, (v, s) in want.items():
                            if seen.get(k, 0) < v:
                                eng.wait_ge(s, v)
                                seen[k] = v
                        ins = fn(eng)
                        if (e, i) in sig and ins is not None:
                            k, s, v, inc = sig[(e, i)]
                            ins.then_inc(s, inc)
                return run
            block.tensor(body(PE))
            block.scalar(body(ACT))
            block.vector(body(DVE))
            block.gpsimd(body(POOL))
            block.sync(body(SP))


D = 1024
TT = 512
N_META = 16
SEQ = 8192
NEG = -30000.0
EPS = 1e-6
NQH = 16

PP_SPEC = [
    ("n0pre", 8), ("n0post", 8), ("f0pre", 8), ("f0post", 8),
    ("n1pre", 8), ("n1post", 8), ("f1pre", 8), ("f1post", 8),
    ("lru_cw", 32), ("lru_cb", 8), ("b_a", 8), ("b_x", 8), ("lam", 8),
    ("f0_cw", 132), ("f0_cb", 44), ("f1_cw", 132), ("f1_cb", 44),
    ("ssm_cw", 128), ("ssm_cb", 32), ("gnorm", 16), ("dskip", 16),
    ("sinks", 16), ("sinkp", 8), ("dt_bias", 32), ("a_log", 32),
]
CC_SPEC = [("ident", 128), ("tri", 128), ("stri", 128), ("abdp", 4096),
           ("Dm", 256), ("maskm", 256), ("nslope", 16), ("nb128", 16)]


def _offsets(spec):
    off, o = {}, 0
    for n, c in spec:
        off[n] = o
        o += c
    return off, o


PP_OFF, NPP = _offsets(PP_SPEC)
CC_OFF, NCC = _offsets(CC_SPEC)


def _chunk(v):
    v = np.asarray(v, np.float32)
    return np.ascontiguousarray(v.reshape(-1, 128).T)


def _row(v):
    v = np.asarray(v, np.float32)
    return np.ascontiguousarray(np.broadcast_to(v[None, :], (128, v.shape[0])))


def pack_params(inp):
    pp = np.zeros((128, NPP), np.float32)

    def put(name, arr):
        pp[:, PP_OFF[name]:PP_OFF[name] + arr.shape[1]] = arr
    put("n0pre", _chunk(inp["l0_mix_pre_norm"])); put("n0post", _chunk(inp["l0_mix_post_norm"]))
    put("f0pre", _chunk(inp["l0_ffn_pre_norm"])); put("f0post", _chunk(inp["l0_ffn_post_norm"]))
    put("n1pre", _chunk(inp["l1_mix_pre_norm"])); put("n1post", _chunk(inp["l1_mix_post_norm"]))
    put("f1pre", _chunk(inp["l1_ffn_pre_norm"])); put("f1post", _chunk(inp["l1_ffn_post_norm"]))
    put("lru_cw", np.concatenate([_chunk(inp["l0_lru_conv_w"][k]) for k in range(4)], axis=1))
    put("lru_cb", _chunk(inp["l0_lru_conv_b"]))
    put("b_a", _chunk(inp["l0_lru_b_a"])); put("b_x", _chunk(inp["l0_lru_b_x"])); put("lam", _chunk(inp["l0_lru_lambda"]))
    for l in (0, 1):
        put(f"f{l}_cw", np.concatenate([_chunk(inp[f"l{l}_ffn_conv_w"][k]) for k in range(3)], axis=1))
        put(f"f{l}_cb", _chunk(inp[f"l{l}_ffn_conv_b"]))
    put("ssm_cw", np.concatenate([_chunk(inp["l1_ssm_conv_w"][k]) for k in range(4)], axis=1))
    put("ssm_cb", _chunk(inp["l1_ssm_conv_b"]))
    put("gnorm", _chunk(inp["l1_gate_norm"]))
    put("dskip", _chunk(np.repeat(np.asarray(inp["l1_d_skip"], np.float32), 64)))
    put("sinkp", _chunk(np.repeat(np.asarray(inp["l0_attn_sinks"], np.float32), 64)))
    put("sinks", _row(inp["l0_attn_sinks"])); put("dt_bias", _row(inp["l1_dt_bias"])); put("a_log", _row(inp["l1_a_log"]))
    return pp


def make_consts():
    cc = np.zeros((128, NCC), np.float32)

    def put(name, arr):
        cc[:, CC_OFF[name]:CC_OFF[name] + arr.shape[1]] = arr
    idx = np.arange(128)
    put("ident", np.eye(128, dtype=np.float32))
    put("tri", (idx[:, None] <= idx[None, :]).astype(np.float32))
    put("stri", (idx[None, :] < idx[:, None]).astype(np.float32))
    slopes = (2.0 ** (-8.0 * np.arange(1, NQH + 1, dtype=np.float32) / NQH)).astype(np.float32)
    tk = idx[:, None].astype(np.float32)
    tq = idx[None, :].astype(np.float32)
    dd = tq - tk
    dp = tq + 128.0 - tk
    abd = np.zeros((128, NQH, 128), np.float32)
    abp = np.zeros((128, NQH, 128), np.float32)
    for h in range(NQH):
        abd[:, h, :] = np.where(dd >= 0, -slopes[h] * dd, NEG)
        abp[:, h, :] = np.where(dp < 128, -slopes[h] * dp, NEG)
    abdp = np.zeros((128, 8, 4, 128), np.float32)
    for hp in range(8):
        abdp[:, hp, 0], abdp[:, hp, 1] = abd[:, 2 * hp], abd[:, 2 * hp + 1]
        abdp[:, hp, 2], abdp[:, hp, 3] = abp[:, 2 * hp], abp[:, 2 * hp + 1]
    put("abdp", abdp.reshape(128, -1))
    t = np.arange(256)[None, :].astype(np.float32)
    s_ = idx[:, None].astype(np.float32)
    dm = np.minimum(t - s_, 128.0)
    ok = (t - s_) >= 0
    put("Dm", np.where(ok, dm, 0.0).astype(np.float32))
    put("maskm", np.where(ok, 0.0, NEG).astype(np.float32))
    put("nslope", _row(-slopes)); put("nb128", _row(-128.0 * slopes))
    return cc


class Rot:
    def __init__(self, items):
        self.items = list(items)
        self.i = 0

    def get(self):
        t = self.items[self.i % len(self.items)]
        self.i += 1
        return t


def build(NT, stages=4, nslot=5):
    nc = bass.Bass("TRN2", target_bir_lowering=False)
    S = Sched(nc)
    Tn = NT * TT
    dram = lambda n, sh, kind="ExternalInput": nc.dram_tensor(n, list(sh), F32, kind=kind).ap()
    xT_d = dram("xT", [D, Tn])
    out_d = dram("outT", [D, Tn], "ExternalOutput")
    pp_d = dram("pp", [128, NPP])
    cc_d = dram("cc", [128, NCC])
    W = {n: dram(n, sh) for n, sh in [
        ("l0_w_in", [1024, 3328]), ("l0_w_out", [2048, 1024]), ("l0_lru_w_a", [8, 128, 128]), ("l0_lru_w_x", [8, 128, 128]),
        ("l0_ffn_w_up", [1024, 5632]), ("l0_ffn_w_down", [2816, 1024]),
        ("l1_w_in", [1024, 6176]), ("l1_w_out", [2048, 1024]), ("l1_ffn_w_up", [1024, 5632]), ("l1_ffn_w_down", [2816, 1024])]}
    outT = S.track(out_d, "outT_dram")

    pp = S.sb("pp_s", [128, NPP], F32)
    cc = S.sb("cc_s", [128, NCC], F32)
    P = lambda n, c=0, w=1: pp[:, PP_OFF[n] + c:PP_OFF[n] + c + w]
    C = lambda n, a=0, b=None: cc[:, CC_OFF[n] + a:CC_OFF[n] + (dict(CC_SPEC)[n] if b is None else b)]
    drv = S.sb("drv", [128, 8 + 32 + 16 + 8], F32)
    esinkp = lambda hp: drv[:, 56 + hp:57 + hp]
    m8sp = lambda c: drv[:, c:c + 1]
    Aneg = drv[:, 8:40]
    esink = lambda h: drv[:, 40 + h:41 + h]
    ones_bf = S.sb("ones_bf", [128, 128], BF16)
    ones_f = S.sb("ones_f", [128, 128], F32)
    ident_bf = S.sb("ident_bf", [128, 128], BF16)
    wa_bf = S.sb("wa_bf", [128, 8, 128], BF16)
    wx_bf = S.sb("wx_bf", [128, 8, 128], BF16)
    wdt_bf = S.sb("wdt_bf", [128, 8, 32], BF16)
    h = [S.sb(f"h{c}", [128, TT], F32) for c in range(8)]
    xn = [S.sb(f"xn{c}", [128, TT], BF16) for c in range(8)]
    NBFA, NBFT, NFT = 46, 6, 14
    bfa_h = nc.alloc_sbuf_tensor("bfa", [128, NBFA * TT], BF16)
    BFA = [S.track(bfa_h[:, i * TT:(i + 1) * TT], f"bfa{i}") for i in range(NBFA)]
    bfm = lambda i, n: mv(BFA[i:i + n], bfa_h[:, i * TT:(i + n) * TT])
    bft = Rot([S.sb(f"bft{i}", [128, TT], BF16) for i in range(NBFT)])
    ft = Rot([S.sb(f"ft{i}", [128, TT], F32) for i in range(NFT)])
    xcb = Rot([S.sb(f"xcb{i}", [128, TT + 4], F32) for i in range(3)])
    ring = [S.sb(f"ring{i}", [128, 8, TT], BF16) for i in range(nslot)]
    PSB = [S.ps(f"psb{i}", [128, TT]) for i in range(8)]
    psr = Rot(PSB[:7])
    pstat = PSB[7]
    sml = Rot([S.sb(f"sml{i}", [128, 128], F32) for i in range(8)])
    lruh_h = nc.alloc_sbuf_tensor("lruh", [128, 8 * 3], F32)
    lruh = [S.track(lruh_h[:, 3 * c:3 * c + 3], f"lruh{c}") for c in range(8)]
    lrus_h = nc.alloc_sbuf_tensor("lrus", [128, 8], F32)
    lrus = [S.track(lrus_h[:, c:c + 1], f"lrus{c}") for c in range(8)]
    kbuf = [S.sb(f"kbuf{g}", [128, 128 + TT], BF16) for g in range(2)]
    vbuf = S.sb("vbuf", [128, 5, 128], BF16)
    kmeta = [S.sb(f"kmeta{g}", [128, 16], BF16) for g in range(2)]
    vmeta = S.sb("vmeta", [128, 128], BF16)
    fh_h = [nc.alloc_sbuf_tensor(f"fh{l}", [128, 44 * 2], F32) for l in range(2)]
    fh = [[S.track(fh_h[l][:, 2 * c:2 * c + 2], f"fh{l}_{c}") for c in range(44)] for l in range(2)]
    sh_h = nc.alloc_sbuf_tensor("sh", [128, 32 * 3], F32)
    sh = [S.track(sh_h[:, 3 * c:3 * c + 3], f"sh{c}") for c in range(32)]
    st_h = nc.alloc_sbuf_tensor("state", [128, 2048], F32)
    state = [S.track(st_h[:, 256 * g:256 * g + 256], f"state{g}") for g in range(8)]
    stb_h = nc.alloc_sbuf_tensor("stbf", [128, 2048], BF16)
    stbf = [S.track(stb_h[:, 256 * g:256 * g + 256], f"stbf{g}") for g in range(8)]
    dts = S.sb("dts", [128, 4, 32], F32)
    dtA = S.sb("dtA", [128, 4, 32], F32)

    S.dmav(SP, pp.v, pp_d[:, :], key=pp)
    S.dmav(SP, cc.v, cc_d[:, :], key=cc)
    S.dmav(POOL, wa_bf.v, W["l0_lru_w_a"].rearrange("n i j -> i n j"), key=wa_bf)
    S.dmav(POOL, wx_bf.v, W["l0_lru_w_x"].rearrange("n i j -> i n j"), key=wx_bf)
    S.dmav(POOL, wdt_bf.v, W["l1_w_in"][:, 6144:6176].rearrange("(k p) n -> p k n", p=128), key=wdt_bf)
    S.memset(ones_bf.v, 1.0)
    S.memset(ones_f.v, 1.0)
    S.copy(ident_bf.v, C("ident"))
    for t_ in lruh + lrus + sh + state + stbf + [x for l in fh for x in l] + kbuf + kmeta + [vbuf, vmeta]:
        S.memset(t_.v, 0.0)
    t0_ = sml.get()
    S.act(t0_[:, 0:8], P("lam", 0, 8), AF.Exp, scale=-1.0)
    S.act(t0_[:, 8:16], t0_[:, 0:8], AF.Ln, bias=1.0)
    S.ts(drv[:, 0:8], t0_[:, 8:16], -8.0, ALU.mult)
    S.act(t0_[:, 16:48], P("a_log", 0, 32), AF.Exp)
    S.ts(drv[:, 8:40], t0_[:, 16:48], -1.0, ALU.mult)
    S.act(drv[:, 40:56], P("sinks", 0, 16), AF.Exp)
    S.act(drv[:, 56:64], P("sinkp", 0, 8), AF.Exp)

    ucount = [0]

    ucache = {}

    def load_unit(wname, r0, nk, c0, ncol):
        slot = ring[ucount[0] % nslot]
        ucount[0] += 1
        key = (wname, r0, nk, c0, ncol)
        if key not in ucache:
            nm = f"U{len(ucache)}"
            uh = nc.dram_tensor(nm, [128, nk * ncol], BF16, kind="Internal").ap()
            ut = S.track(uh, nm)
            first = True
            for k0 in range(0, nk, 4):
                k1 = min(nk, k0 + 4)
                S.op(POOL, lambda e, uh=uh, k0=k0, k1=k1: e.dma_start(
                    out=uh[:, k0 * ncol:k1 * ncol].rearrange("p (k n) -> p k n", n=ncol),
                    in_=W[wname][r0 + k0 * 128:r0 + k1 * 128, c0:c0 + ncol].rearrange("(k p) n -> p k n", p=128)),
                    writes=[ut], acc=not first, dma_tile=ut)
                first = False
            ucache[key] = (uh, ut)
        uh, ut = ucache[key]
        S.op(SP, lambda e, uh=uh: e.dma_start(out=slot.h[:, 0:nk, 0:ncol], in_=uh[:, :].rearrange("p (k n) -> p k n", n=ncol)),
             reads=[ut], writes=[slot], dma_tile=slot)
        return slot

    def rmsnorm_to_xn(wname):
        for c in range(8):
            sq = bft.get()
            S.act(sq.v, h[c].v, AF.Square)
            S.mm(pstat.v, ones_bf.v, sq.v, start=(c == 0), stop=(c == 7))
        rs = ft.get()
        S.act(rs.v, pstat.v, AF.Sqrt, bias=EPS_AP(), scale=1.0 / D)
        S.recip(rs.v, rs.v)
        for c in range(8):
            S.stt(xn[c].v, h[c].v, P(wname, c), rs.v, ALU.mult, ALU.mult)

    eps_t = S.sb("eps_t", [128, 1], F32)
    S.memset(eps_t.v, EPS)
    EPS_AP = lambda: eps_t[:, 0:1]

    class Epi:
        def __init__(self, wname):
            self.w = wname
            self.mo = []

        def chunk(self, oc, ps):
            mo = ft.get()
            S.copy(mo.v, ps.v, eng=ACT)
            sq = bft.get()
            S.act(sq.v, ps.v, AF.Square)
            S.mm(pstat.v, ones_bf.v, sq.v, start=(oc == 0), stop=(oc == 7))
            self.mo.append(mo)

        def finish(self):
            rs = ft.get()
            S.act(rs.v, pstat.v, AF.Sqrt, bias=EPS_AP(), scale=1.0 / D)
            S.recip(rs.v, rs.v)
            for c in range(8):
                S.stt(self.mo[c].v, self.mo[c].v, P(self.w, c), rs.v, ALU.mult, ALU.mult)
                S.tt(h[c].v, h[c].v, self.mo[c].v, ALU.add)

    def conv_chunk(ps, halo, K, wname, bname, cidx, nch):
        xb = xcb.get()
        S.copy(xb[:, 0:K - 1], halo.v, eng=POOL)
        S.copy(xb[:, K - 1:K - 1 + TT], ps.v, eng=ACT)
        S.copy(halo.v, xb[:, TT:TT + K - 1], eng=POOL)
        cv = ft.get()
        wcol = lambda k: P(wname, k * nch + cidx)
        S.act(cv.v, ps.v, AF.Identity, bias=P(bname, cidx), scale=wcol(K - 1))
        for k in range(K - 2, -1, -1):
            S.stt(cv.v, xb[:, k:k + TT], wcol(k), cv.v, ALU.mult, ALU.add)
        return cv

    def proj_chunk(slot_of_k, j, rhs_list, ncolj=128):
        ps = psr.get()
        n = len(rhs_list)
        for k in range(n):
            sl, kk = slot_of_k(k)
            S.mm(ps[0:ncolj, :], sl[:, kk, j * 128:j * 128 + ncolj], rhs_list[k].v, start=(k == 0), stop=(k == n - 1))
        return ps

    def l0_mixer(ti):
        rmsnorm_to_xn("n0pre")
        gate, q, ya, yb = BFA[0:8], BFA[8:16], BFA[16:24], BFA[24:32]
        for u in range(7):
            ncol = 512 if u < 6 else 256
            slot = load_unit("l0_w_in", 0, 8, 512 * u, ncol)
            sk = lambda k, slot=slot: (slot, k)
            if u < 6:
                for j in range(4):
                    c = (u % 2) * 4 + j
                    ps = proj_chunk(sk, j, xn)
                    if u < 2:
                        S.act(gate[c].v, ps.v, AF.Gelu_apprx_tanh)
                    elif u < 4:
                        lru_chunk(c, ps, gate[c], ya[c])
                    else:
                        S.copy(q[c].v, ps.v, eng=ACT)
            else:
                for g in range(2):
                    ps = psr.get()
                    for half in range(2):
                        for k in range(8):
                            S.mm(ps[64 * half:64 * half + 64, :], slot[:, k, 64 * g:64 * g + 64], xn[k].v,
                                 start=(k == 0), stop=(k == 7), acc=not (half == 0 and k == 0))
                    S.copy(kbuf[g][:, 128:128 + TT], ps.v, eng=ACT)
                    if ti == 0:
                        S.copy(kmeta[g].v, kbuf[g][:, 128:144])
                ps = psr.get()
                for blk in range(4):
                    for k in range(8):
                        S.mm(ps[:, blk * 128:blk * 128 + 128], xn[k][:, blk * 128:blk * 128 + 128], slot[:, k, 128:256],
                             start=(k == 0), stop=(k == 7), acc=not (blk == 0 and k == 0))
                S.copy(vbuf[:, 1:5, :], ps.v.r("p (b n) -> p b n", b=4), eng=ACT)
                if ti == 0:
                    S.copy(vmeta[0:16, :], vbuf[0:16, 1, :])
        attention(ti, q, yb)
        for g in range(2):
            S.copy(kbuf[g][:, 0:128], kbuf[g][:, TT:TT + 128], eng=POOL)
        S.copy(vbuf[:, 0, :], vbuf[:, 4, :], eng=POOL)
        epi = Epi("n0post")
        for cg in range(2):
            ua = load_unit("l0_w_out", 0, 8, 512 * cg, 512)
            ub = load_unit("l0_w_out", 1024, 8, 512 * cg, 512)
            sk = lambda k, ua=ua, ub=ub: (ua, k) if k < 8 else (ub, k - 8)
            for j in range(4):
                ps = proj_chunk(sk, j, list(ya) + list(yb))
                epi.chunk(4 * cg + j, ps)
        epi.finish()

    def lru_chunk(c, ps, gate_c, ya_c):
        cv = conv_chunk(ps, lruh[c], 4, "lru_cw", "lru_cb", c, 8)
        xvb = bft.get()
        S.copy(xvb.v, cv.v, eng=ACT)
        pr, pi = psr.get(), psr.get()
        S.mm(pr.v, wa_bf[:, c, :], xvb.v)
        S.mm(pi.v, wx_bf[:, c, :], xvb.v)
        r, i_, a, om = ft.get(), ft.get(), ft.get(), ft.get()
        S.act(r.v, pr.v, AF.Sigmoid, bias=P("b_a", c))
        S.act(i_.v, pi.v, AF.Sigmoid, bias=P("b_x", c))
        S.act(a.v, r.v, AF.Exp, scale=m8sp(c))
        S.act(om.v, a.v, AF.Square)
        S.act(om.v, om.v, AF.Sqrt, scale=-1.0, bias=1.0)
        S.tt(i_.v, i_.v, cv.v, ALU.mult)
        S.tt(i_.v, i_.v, om.v, ALU.mult)
        hs = ft.get()
        S.scan(hs.v, a.v, i_.v, lrus[c].v)
        S.copy(lrus[c].v, hs[:, TT - 1:TT], eng=POOL)
        S.tt(ya_c.v, gate_c.v, hs.v, ALU.mult)

    def attention(ti, q, yb):
        for hp in range(8):
            g = hp // 4
            for blk in range(4):
                n = 4 * ti + blk
                ps, pm, pb = psr.get(), psr.get(), psr.get()
                for e in range(2):
                    b0 = 64 * e
                    qh = q[hp][b0:b0 + 64, blk * 128:blk * 128 + 128]
                    first = (e == 0)
                    S.mm(ps[:, e * 128:e * 128 + 128], kbuf[g][b0:b0 + 64, 128 + blk * 128:256 + blk * 128], qh, acc=not first)
                    S.mm(ps[:, 256 + e * 128:384 + e * 128], kbuf[g][b0:b0 + 64, blk * 128:blk * 128 + 128], qh, acc=True)
                    S.mm(pm[0:16, e * 128:e * 128 + 128], kmeta[g][b0:b0 + 64, 0:16], qh, acc=not first)
                sb = ft.get()
                S.stt(sb.v, ps.v, 0.125, C("abdp", 512 * hp, 512 * hp + 512), ALU.mult, ALU.add)
                pT = bft.get()
                S.act(pT.v, sb.v, AF.Exp)
                if n == 0:
                    S.memset(pT[:, 256:512], 0.0, eng=DVE)
                    S.memset(pT[0:16, 0:256], 0.0, eng=DVE)
                elif n == 1:
                    S.memset(pT[0:16, 256:512], 0.0, eng=DVE)
                pTm = bft.get()
                for e in range(2):
                    hh = 2 * hp + e
                    dstm = pTm[0:16, e * 128:e * 128 + 128]
                    srcm = pm[0:16, e * 128:e * 128 + 128]
                    if n >= 2:
                        S.act(dstm, srcm, AF.Exp, bias=C("nb128", hh, hh + 1)[0:16, :], scale=0.125, acc=(e == 1))
                    else:
                        tm = sml.get()
                        S.stt(tm[0:16, :], C("Dm", n * 128, n * 128 + 128)[0:16, :], C("nslope", hh, hh + 1)[0:16, :],
                              C("maskm", n * 128, n * 128 + 128)[0:16, :], ALU.mult, ALU.add)
                        S.stt(tm[0:16, :], srcm, 0.125, tm[0:16, :], ALU.mult, ALU.add)
                        S.act(dstm, tm[0:16, :], AF.Exp, acc=(e == 1))
                for e in range(2):
                    b0 = 64 * e
                    for o0 in (0, 256):
                        dst = pb[b0:b0 + 64, o0:o0 + 128]
                        if o0 == 0:
                            l1, l2, l3 = vbuf[:, 1 + blk, 64 * g:64 * g + 64], vbuf[:, blk, 64 * g:64 * g + 64], vmeta[0:16, 64 * g:64 * g + 64]
                        else:
                            l1, l2, l3 = ones_bf[:, 0:64], ones_bf[:, 0:64], ones_bf[0:16, 0:64]
                        S.mm(dst, l1, pT[:, e * 128:e * 128 + 128], start=True, stop=False, acc=not (e == 0 and o0 == 0))
                        S.mm(dst, l2, pT[:, 256 + e * 128:384 + e * 128], start=False, stop=False)
                        S.mm(dst, l3, pTm[0:16, e * 128:e * 128 + 128], start=False, stop=True)
                dn = sml.get()
                S.ts(dn.v, pb[:, 256:384], esinkp(hp), ALU.add)
                S.recip(dn.v, dn.v)
                S.tt(yb[hp][:, blk * 128:blk * 128 + 128], pb[:, 0:128], dn.v, ALU.mult, acc=(blk > 0))

    def ffn(l):
        rmsnorm_to_xn(f"f{l}pre")
        actb = BFA[0:22]
        for u in range(11):
            slot = load_unit(f"l{l}_ffn_w_up", 0, 8, 512 * u, 512)
            sk = lambda k, slot=slot: (slot, k)
            for j in range(4):
                ci = 4 * u + j
                ps = proj_chunk(sk, j, xn)
                cv = conv_chunk(ps, fh[l][ci], 3, f"f{l}_cw", f"f{l}_cb", ci, 44)
                if ci < 22:
                    S.act(actb[ci].v, cv.v, AF.Gelu_apprx_tanh)
                else:
                    S.tt(actb[ci - 22].v, actb[ci - 22].v, cv.v, ALU.mult)
        epi = Epi(f"f{l}post")
        for cg in range(2):
            us = [load_unit(f"l{l}_ffn_w_down", 1024 * i, 8 if i < 2 else 6, 512 * cg, 512) for i in range(3)]
            sk = lambda k, us=us: (us[k // 8], k % 8)
            for j in range(4):
                ps = proj_chunk(sk, j, actb)
                epi.chunk(4 * cg + j, ps)
        epi.finish()

    def l1_mixer(ti):
        rmsnorm_to_xn("n1pre")
        xf, Bf, Cf = BFA[0:16], BFA[16:24], BFA[24:32]
        xdt_v, xdtd_v, Btok_v, ytok_v = bfm(32, 4), bfm(36, 4), bfm(40, 2), bfm(42, 4)
        tri, stri = C("tri"), C("stri")
        for blk in range(4):
            ps = psr.get()
            for k in range(8):
                S.mm(ps[:, 0:32], xn[k][:, blk * 128:blk * 128 + 128], wdt_bf[:, k, :], start=(k == 0), stop=(k == 7))
            t1 = sml.get()
            S.tt(t1[:, 0:32], ps[:, 0:32], P("dt_bias", 0, 32), ALU.add)
            S.act(t1[:, 32:64], t1[:, 0:32], AF.Exp)
            S.act(dts[:, blk, :], t1[:, 32:64], AF.Ln, bias=1.0, acc=(blk > 0))
            S.tt(dtA[:, blk, :], dts[:, blk, :], Aneg, ALU.mult, acc=(blk > 0))
        for u in range(8):
            slot = load_unit("l1_w_in", 0, 8, 2048 + 512 * u, 512)
            sk = lambda k, slot=slot: (slot, k)
            for j in range(4):
                cc_ = 4 * u + j
                ps = proj_chunk(sk, j, xn)
                cv = conv_chunk(ps, sh[cc_], 4, "ssm_cw", "ssm_cb", cc_, 32)
                dst = xf[cc_] if cc_ < 16 else (Bf[cc_ - 16] if cc_ < 24 else Cf[cc_ - 24])
                S.act(dst.v, cv.v, AF.Silu)
        for blk in range(4):
            ssd_chunk(blk, xf, Bf, Cf, xdt_v, xdtd_v, Btok_v, ytok_v, tri, stri)
        for u in range(4):
            slot = load_unit("l1_w_in", 0, 8, 512 * u, 512)
            sk = lambda k, slot=slot: (slot, k)
            for j in range(4):
                c = 4 * u + j
                ps = proj_chunk(sk, j, xn)
                sz = ft.get()
                S.act(sz.v, ps.v, AF.Silu)
                S.tt(xf[c].v, xf[c].v, sz.v, ALU.mult)
                if c % 2 == 1:
                    pg = psr.get()
                    for i2, c2 in enumerate((c - 1, c)):
                        sq = bft.get()
                        S.act(sq.v, xf[c2].v, AF.Square)
                        S.mm(pg.v, ones_bf.v, sq.v, start=(i2 == 0), stop=(i2 == 1))
                    rs = ft.get()
                    S.act(rs.v, pg.v, AF.Sqrt, bias=EPS_AP(), scale=1.0 / 256)
                    S.recip(rs.v, rs.v)
                    for c2 in (c - 1, c):
                        S.stt(xf[c2].v, xf[c2].v, P("gnorm", c2), rs.v, ALU.mult, ALU.mult)
        epi = Epi("n1post")
        for cg in range(2):
            us = [load_unit("l1_w_out", 1024 * i, 8, 512 * cg, 512) for i in range(2)]
            sk = lambda k, us=us: (us[k // 8], k % 8)
            for j in range(4):
                ps = proj_chunk(sk, j, xf)
                epi.chunk(4 * cg + j, ps)
        epi.finish()

    def ssd_chunk(blk, xf, Bf, Cf, xdt_v, xdtd_v, Btok_v, ytok_v, tri, stri):
        cs_ = slice(blk * 128, blk * 128 + 128)
        dt_b = dts[:, blk, :]
        dtA_b = dtA[:, blk, :]
        for half in range(2):
            pt = psr.get()
            ptb = pt.v.bitcast(BF16)
            for c in range(8):
                S.tr(ptb[:, c * 128:c * 128 + 128], xf[8 * half + c][:, cs_], ident_bf.v, acc=(c > 0))
            S.tt(xdt_v[:, half * 1024:half * 1024 + 1024].r("p (h d) -> p h d", d=64), ptb.r("p (h d) -> p h d", d=64),
                 dt_b[:, 16 * half:16 * half + 16].us(2).bc([128, 16, 64]), ALU.mult, acc=(half == 1))
        pt = psr.get()
        ptb = pt.v.bitcast(BF16)
        for g in range(8):
            S.tr(ptb[:, g * 128:g * 128 + 128], Bf[g][:, cs_], ident_bf.v, acc=(g > 0))
        S.copy(Btok_v, ptb, eng=ACT)
        pc = psr.get()
        S.mm(pc[:, 0:32], tri, dtA_b)
        S.mm(pc[:, 32:64], ones_f.v, dtA_b, acc=True)
        sm = sml.get()
        S.act(sm[:, 0:64], pc[:, 0:64], AF.Exp)
        ecs, cdec = sm[:, 0:32], sm[:, 32:64]
        d2e = sml.get()
        ycols = ytok_v
        for half in range(2):
            pcb = psr.get()
            for gg in range(4):
                g = 4 * half + gg
                S.mm(pcb[:, gg * 128:gg * 128 + 128], Bf[g][:, cs_], Cf[g][:, cs_], acc=(gg > 0))
            cbm = ft.get()
            S.tt(cbm.v.r("p (g l) -> p g l", g=4), pcb.v.r("p (g l) -> p g l", g=4), tri.us(1).bc([128, 4, 128]), ALU.mult)
            for gg in range(4):
                g = 4 * half + gg
                lt = ft.get()
                S.tt(lt.v.r("p (h s) -> p h s", h=4), stri.us(1).bc([128, 4, 128]),
                     dtA_b[:, 4 * g:4 * g + 4].us(2).bc([128, 4, 128]), ALU.mult)
                pseg = psr.get()
                for hh in range(4):
                    S.mm(pseg[:, hh * 128:hh * 128 + 128], lt[:, hh * 128:hh * 128 + 128], tri, acc=(hh > 0))
                es = ft.get()
                S.act(es.v, pseg.v, AF.Exp)
                S.copy(d2e[:, 4 * g:4 * g + 4], es.v.r("p (h l) -> p h l", h=4)[:, :, 127], eng=POOL, acc=(g > 0))
                mt = bft.get()
                S.tt(mt.v.r("p (h l) -> p h l", h=4), es.v.r("p (h l) -> p h l", h=4),
                     cbm[:, gg * 128:gg * 128 + 128].us(1).bc([128, 4, 128]), ALU.mult)
                py = psr.get()
                for hh in range(4):
                    hd = 4 * g + hh
                    S.mm(py[:, hh * 64:hh * 64 + 64], mt[:, hh * 128:hh * 128 + 128], xdt_v[:, hd * 64:hd * 64 + 64], acc=(hh > 0))
                S.mm(py[:, 256:512], Cf[g][:, cs_], stbf[g].v, acc=True)
                yo = ft.get()
                S.tt(yo[:, 0:256].r("p (h d) -> p h d", h=4), py[:, 256:512].r("p (h d) -> p h d", h=4),
                     ecs[:, 4 * g:4 * g + 4].us(2).bc([128, 4, 64]), ALU.mult)
                S.tt(ycols[:, 256 * g:256 * g + 256], py[:, 0:256], yo[:, 0:256], ALU.add, acc=(g > 0))
        S.tt(xdtd_v.r("p (h d) -> p h d", d=64), xdt_v.r("p (h d) -> p h d", d=64), d2e[:, 0:32].us(2).bc([128, 32, 64]), ALU.mult)
        for g in range(8):
            pst = psr.get()
            S.mm(pst[:, 0:256], Btok_v[:, g * 128:g * 128 + 128], xdtd_v[:, 256 * g:256 * g + 256])
            S.tt(state[g].v.r("p (h d) -> p h d", h=4), state[g].v.r("p (h d) -> p h d", h=4),
                 cdec[:, 4 * g:4 * g + 4].us(2).bc([128, 4, 64]), ALU.mult)
            S.tt(state[g].v, state[g].v, pst[:, 0:256], ALU.add)
            S.copy(stbf[g].v, state[g].v, eng=ACT)
        for half in range(2):
            pt = psr.get()
            ptb = pt.v.bitcast(BF16)
            for c in range(8):
                cc_ = 8 * half + c
                S.tr(ptb[:, c * 128:c * 128 + 128], ycols[:, cc_ * 128:cc_ * 128 + 128], ident_bf.v, acc=(c > 0))
            for c in range(8):
                cc_ = 8 * half + c
                S.stt(xf[cc_][:, cs_], xf[cc_][:, cs_], P("dskip", cc_), ptb[:, c * 128:c * 128 + 128], ALU.mult, ALU.add)

    for ti in range(NT):
        t0 = ti * TT
        for c in range(8):
            S.dmav(SP, h[c].v, xT_d[c * 128:(c + 1) * 128, t0:t0 + TT], key=h[c])
        if stages >= 1:
            l0_mixer(ti)
        if stages >= 2:
            ffn(0)
        if stages >= 3:
            l1_mixer(ti)
        if stages >= 4:
            ffn(1)
        for c in range(8):
            S.op(SP, lambda e, c=c, t0=t0: e.dma_start(out=out_d[c * 128:(c + 1) * 128, t0:t0 + TT], in_=h[c].h[:]),
                 reads=[h[c]], writes=[outT], acc=True, dma_tile=h[c])
    S.op(SP, lambda e: None, reads=[outT])
    S.emit()
    nc._sched_stats = (S.sem_max, S.n_ops)
    return nc


def prepare_inputs(inputs, NT):
    Tn = NT * TT
    x = np.asarray(inputs["x"], np.float32)
    meta = np.asarray(inputs["meta_tokens"], np.float32)
    B = x.shape[0]
    pp = pack_params(inputs)
    cc = make_consts()
    wnames = ["l0_w_in", "l0_w_out", "l0_lru_w_a", "l0_lru_w_x", "l0_ffn_w_up", "l0_ffn_w_down",
              "l1_w_in", "l1_w_out", "l1_ffn_w_up", "l1_ffn_w_down"]
    wts = {n: np.ascontiguousarray(np.asarray(inputs[n], np.float32)) for n in wnames}
    maps = []
    nreal = min(x.shape[1], Tn - N_META)
    for b in range(B):
        xT = np.zeros((D, Tn), np.float32)
        xT[:, :N_META] = meta.T
        xT[:, N_META:N_META + nreal] = x[b, :nreal].T
        m = {"xT": xT, "pp": pp, "cc": cc}
        m.update(wts)
        maps.append(m)
    return maps, nreal


NT_FULL = (N_META + SEQ + TT - 1) // TT


def kernel(**inputs):
    maps, nreal = prepare_inputs(inputs, NT_FULL)
    nc = build(NT_FULL)
    res = run_bass_kernel_spmd(nc, maps, core_ids=list(range(len(maps))))
    out = np.stack([np.ascontiguousarray(r["outT"][:, N_META:N_META + nreal].T) for r in res.results], axis=0)
    return out.astype(np.float32)
```
